# Optimizing a Trainium2 kernel written in Bass

```python
import math
import jax, jax.numpy as jnp
from jax import lax
import numpy as np

D_MODEL = 1024
BATCH = 8
SEQ = 2048
DEPTH = 4
DEC_BATCH = 128
DEC_SEQ = 8
PAST_LEN = 16384
PAGE_SIZE = 128

H_A = 8
DK_A = 128
DV_A = 128
QK_A = H_A * DK_A
D_A = H_A * DV_A
CHUNK = 64
D_B = 1024
CONV_W = 31
PLE_DIM = 256
EPS = 1e-6
SPLIT_SIZES = (QK_A, QK_A, D_A, D_A, D_B, D_B, D_B, D_MODEL, D_MODEL)
N_IN = sum(SPLIT_SIZES)
SPLIT_POINTS = tuple(int(v) for v in np.cumsum(SPLIT_SIZES)[:-1])

kernel_name = "hybrid_hgrn2_conformer_step"


def _rmsnorm(x, g):
    xf = x.astype(jnp.float32)
    y = xf * lax.rsqrt(jnp.mean(xf * xf, axis=-1, keepdims=True) + EPS)
    return (y * g.astype(jnp.float32)).astype(x.dtype)


def _layernorm(x, g, b):
    xf = x.astype(jnp.float32)
    mu = jnp.mean(xf, axis=-1, keepdims=True)
    var = jnp.mean(jnp.square(xf - mu), axis=-1, keepdims=True)
    y = (xf - mu) * lax.rsqrt(var + EPS)
    return (y * g.astype(jnp.float32) + b.astype(jnp.float32)).astype(x.dtype)


def _hgrn2_chunked(q, f, v, s0):
    B, T, H, K = q.shape
    V = v.shape[-1]
    c = min(CHUNK, T)
    n = -(-T // c)
    pad = n * c - T
    k = 1.0 - f
    log_f = jnp.log(jnp.maximum(f, 1e-30))
    if pad:
        pw = ((0, 0), (0, pad), (0, 0), (0, 0))
        q, log_f, k, v = (jnp.pad(a, pw) for a in (q, log_f, k, v))

    def to_chunks(a):
        return a.reshape(B, n, c, H, a.shape[-1]).transpose(1, 0, 3, 2, 4)

    mask = jnp.tril(jnp.ones((c, c), dtype=bool))[:, :, None]

    def step(S, inp):
        qc, lfc, kc, vc = inp
        b = jnp.cumsum(lfc, axis=-2)
        diff = b[..., :, None, :] - b[..., None, :, :]
        decay = jnp.exp(jnp.where(mask, diff, -jnp.inf))
        attn = jnp.einsum('bhtk,bhsk,bhtsk->bhts', qc, kc, decay)
        o = (jnp.einsum('bhts,bhsv->bhtv', attn, vc)
             + jnp.einsum('bhtk,bhkv->bhtv', qc * jnp.exp(b), S))
        b_last = b[..., -1:, :]
        S_new = (jnp.exp(b_last[..., 0, :])[..., None] * S
                 + jnp.einsum('bhsk,bhsv->bhkv', kc * jnp.exp(b_last - b), vc))
        return S_new, o

    S, o = lax.scan(step, s0, (to_chunks(q), to_chunks(log_f), to_chunks(k), to_chunks(v)))
    o = o.transpose(1, 0, 3, 2, 4).reshape(B, n * c, H, V)[:, :T]
    return o, S


def _layer(h, p_l, s0, conv_buf, lb, norm_mix, w_in, b_in, gnorm_a, w_br_a, conv_w, conv_b,
           ln_g, ln_b, w_br_b, w_o, w_ple, w_ple_gate, norm_ple):
    B, T, _ = h.shape
    xn = _rmsnorm(h, norm_mix)
    proj = xn @ w_in + b_in
    q, fz, iv, za, ga, gb, zb, ma, mb = jnp.split(proj, SPLIT_POINTS, axis=-1)

    f = lb + (1.0 - lb) * jax.nn.sigmoid(fz.astype(jnp.float32))
    qf = jax.nn.silu(q.astype(jnp.float32)).reshape(B, T, H_A, DK_A)
    o, s_new = _hgrn2_chunked(qf, f.reshape(B, T, H_A, DK_A),
                              iv.astype(jnp.float32).reshape(B, T, H_A, DV_A),
                              s0.astype(jnp.float32))
    o = _rmsnorm(o, gnorm_a.reshape(H_A, DV_A)).reshape(B, T, D_A).astype(h.dtype)
    y_a = (o * jax.nn.silu(za)) @ w_br_a

    u = ga * jax.nn.sigmoid(gb)
    u_full = jnp.concatenate([conv_buf.astype(u.dtype), u], axis=1)
    cv = lax.conv_general_dilated(u_full, conv_w[:, None, :], window_strides=(1,), padding='VALID',
                                  dimension_numbers=('NWC', 'WIO', 'NWC'),
                                  feature_group_count=D_B) + conv_b
    cv = jax.nn.silu(_layernorm(cv, ln_g, ln_b))
    y_b = (cv * jax.nn.silu(zb)) @ w_br_b

    merged = jax.nn.sigmoid(ma) * y_a + jax.nn.sigmoid(mb) * y_b
    h = h + merged @ w_o

    gate = jax.nn.sigmoid(_rmsnorm(h, norm_ple) @ w_ple_gate)
    h = h + gate * (p_l @ w_ple)
    return h, s_new.astype(h.dtype), u_full[:, -(CONV_W - 1):]


def setup_inputs(seed: int = 0) -> dict:
    key = jax.random.key(seed)
    ks = jax.random.split(key, 24)
    nrm = jax.random.normal
    f32 = jnp.float32
    return {
        "x_prompt": nrm(ks[0], (BATCH, SEQ, D_MODEL), f32),
        "x_sample": nrm(ks[1], (DEC_BATCH, DEC_SEQ, D_MODEL), f32),
        "state_hgrn": 0.5 * nrm(ks[2], (DEPTH, DEC_BATCH, H_A, DK_A, DV_A), f32),
        "state_conv": 0.5 * nrm(ks[3], (DEPTH, DEC_BATCH, CONV_W - 1, D_B), f32),
        "p_prompt": nrm(ks[4], (DEPTH, BATCH, SEQ, PLE_DIM), f32),
        "p_sample": nrm(ks[5], (DEPTH, DEC_BATCH, DEC_SEQ, PLE_DIM), f32),
        "lb_param": nrm(ks[6], (DEPTH, QK_A), f32),
        "norm_mix": 1.0 + 0.01 * nrm(ks[7], (DEPTH, D_MODEL), f32),
        "w_in": nrm(ks[8], (DEPTH, D_MODEL, N_IN), f32) * D_MODEL ** -0.5,
        "b_in": 0.01 * nrm(ks[9], (DEPTH, N_IN), f32),
        "gnorm_a": 1.0 + 0.01 * nrm(ks[10], (DEPTH, D_A), f32),
        "w_br_a": nrm(ks[11], (DEPTH, D_A, D_MODEL), f32) * D_A ** -0.5,
        "conv_w": nrm(ks[12], (DEPTH, CONV_W, D_B), f32) * CONV_W ** -0.5,
        "conv_b": 0.01 * nrm(ks[13], (DEPTH, D_B), f32),
        "ln_g": 1.0 + 0.01 * nrm(ks[14], (DEPTH, D_B), f32),
        "ln_b": 0.01 * nrm(ks[15], (DEPTH, D_B), f32),
        "w_br_b": nrm(ks[16], (DEPTH, D_B, D_MODEL), f32) * D_B ** -0.5,
        "w_o": nrm(ks[17], (DEPTH, D_MODEL, D_MODEL), f32) * D_MODEL ** -0.5,
        "w_ple": nrm(ks[18], (DEPTH, PLE_DIM, D_MODEL), f32) * PLE_DIM ** -0.5,
        "w_ple_gate": nrm(ks[19], (DEPTH, D_MODEL, D_MODEL), f32) * D_MODEL ** -0.5,
        "norm_ple": 1.0 + 0.01 * nrm(ks[20], (DEPTH, D_MODEL), f32),
        "norm_final": 1.0 + 0.01 * nrm(ks[21], (D_MODEL,), f32),
    }


def reference(x_prompt, x_sample, state_hgrn, state_conv, p_prompt, p_sample, lb_param, norm_mix,
              w_in, b_in, gnorm_a, w_br_a, conv_w, conv_b, ln_g, ln_b, w_br_b, w_o, w_ple,
              w_ple_gate, norm_ple, norm_final):
    lbs = jnp.cumsum(jax.nn.softmax(lb_param.astype(jnp.float32), axis=0), axis=0)
    lbs = lbs - lbs[0:1]
    bp = x_prompt.shape[0]
    hp, hs = x_prompt, x_sample
    hgrn_p, conv_p, hgrn_s, conv_s = [], [], [], []
    for l in range(DEPTH):
        params = (lbs[l], norm_mix[l], w_in[l], b_in[l], gnorm_a[l], w_br_a[l], conv_w[l], conv_b[l],
                  ln_g[l], ln_b[l], w_br_b[l], w_o[l], w_ple[l], w_ple_gate[l], norm_ple[l])
        s0_p = jnp.zeros((bp, H_A, DK_A, DV_A), jnp.float32)
        buf_p = jnp.zeros((bp, CONV_W - 1, D_B), x_prompt.dtype)
        hp, sp, cp = _layer(hp, p_prompt[l], s0_p, buf_p, *params)
        hs, ss, cs = _layer(hs, p_sample[l], state_hgrn[l], state_conv[l], *params)
        hgrn_p.append(sp)
        conv_p.append(cp)
        hgrn_s.append(ss)
        conv_s.append(cs)
    y_prompt = _rmsnorm(hp, norm_final)
    y_sample = _rmsnorm(hs, norm_final)
    return (y_prompt, y_sample, jnp.stack(hgrn_p), jnp.stack(conv_p), jnp.stack(hgrn_s), jnp.stack(conv_s))
```

```python
import contextlib
import numpy as np
import concourse.bass as bass
import concourse.mybir as mybir
from concourse.bass_utils import run_bass_kernel_spmd

F32 = mybir.dt.float32
F32R = mybir.dt.float32r
BF16 = mybir.dt.bfloat16
U32 = mybir.dt.uint32
AF = mybir.ActivationFunctionType
ALU = mybir.AluOpType

D = 1024
KC = 8
NIN = 9216
NH = 8
DEPTH = 4
CWID = 31
HALO = 30
PLE = 256
NCORE = 8
TP = 2048
TS = 128
NT = TP + TS
EPS = 1e-6
OFF = dict(q=0, f=1024, iv=2048, za=3072, ga=4096, gb=5120, zb=6144, ma=7168, mb=8192)
PV = dict(norm_mix=0, gnorm_a=1, conv_b=2, ln_g=3, ln_b=4, norm_ple=5)


class Rec:
    __slots__ = ("w", "r")

    def __init__(self):
        self.w = {}
        self.r = {}


def _mx(d, k, v):
    if d.get(k, 0) < v:
        d[k] = v


class V:
    __slots__ = ("buf", "key", "ap")

    def __init__(self, buf, key, ap):
        self.buf = buf
        self.key = key
        self.ap = ap


class Buf:
    def __init__(self, name, t, psum=False):
        self.name = name
        self.t = t
        self.recs = {}
        self.all = Rec()
        self.psum = psum

    def merge(self):
        for r in self.recs.values():
            for s, v in r.w.items():
                _mx(self.all.w, s, v)
            for s, v in r.r.items():
                _mx(self.all.r, s, v)
        self.recs = {}

    def v(self, ap, key=None):
        return V(self, None if self.psum else key, ap)

    def __getitem__(self, idx):
        return V(self, None, self.t[idx])

    def k(self, key, idx):
        return V(self, None if self.psum else key, self.t[idx])


class Eng:
    def __init__(self, name, e, sem, semkey):
        self.name = name
        self.e = e
        self.sem = sem
        self.semkey = semkey
        self.n = 0
        self.waited = {}


class Queue:
    def __init__(self, eng, sems, keys):
        self.eng = eng
        self.sems = sems
        self.keys = keys
        self.n = 0


class KB:
    def __init__(self, nc, es):
        self.nc = nc
        self.es = es
        self.semh = {}
        self.nsem = 0
        self.PE = self._eng("pe", nc.tensor)
        self.ACT = self._eng("act", nc.scalar)
        self.DVE = self._eng("dve", nc.vector)
        self.POOL = self._eng("pool", nc.gpsimd)
        self.SP = Eng("sp", nc.sync, None, None)
        self.qsp = self._queue(self.SP, 24)
        self.qpool = self._queue(self.POOL, 16)
        self.nops = 0
        self.nwaits = 0

    def _newsem(self, name):
        s = self.es.enter_context(self.nc.semaphore(name))
        k = self.nsem
        self.nsem += 1
        self.semh[k] = s
        return s, k

    def _eng(self, name, e):
        s, k = self._newsem("sem_" + name)
        return Eng(name, e, s, k)

    def _queue(self, eng, n):
        sems, keys = [], []
        for i in range(n):
            s, k = self._newsem(f"dq_{eng.name}_{i}")
            sems.append(s)
            keys.append(k)
        return Queue(eng, sems, keys)

    def sb(self, name, shape, dt):
        return Buf(name, self.es.enter_context(self.nc.sbuf_tensor("s_" + name, list(shape), dt)))

    def ps(self, name, shape, dt=F32):
        return Buf(name, self.es.enter_context(self.nc.psum_tensor("p_" + name, list(shape), dt)), psum=True)

    def _need(self, reads, writes, ownkey):
        need = {}
        for v in reads:
            b = v.buf
            for s, val in b.all.w.items():
                _mx(need, s, val)
            if b.psum:
                for s, val in b.all.r.items():
                    if s != ownkey:
                        _mx(need, s, val)
            if v.key is None:
                for r in b.recs.values():
                    for s, val in r.w.items():
                        _mx(need, s, val)
            else:
                r = b.recs.get(v.key)
                if r is not None:
                    for s, val in r.w.items():
                        _mx(need, s, val)
        for v in writes:
            b = v.buf
            lst = [b.all]
            if v.key is None:
                lst += list(b.recs.values())
            else:
                r = b.recs.get(v.key)
                if r is not None:
                    lst.append(r)
            for r in lst:
                for s, val in r.w.items():
                    if s != ownkey:
                        _mx(need, s, val)
                for s, val in r.r.items():
                    if s != ownkey:
                        _mx(need, s, val)
        return need

    def _commit(self, reads, writes, tk, tv):
        for v in reads:
            b = v.buf
            if v.key is None:
                _mx(b.all.r, tk, tv)
            else:
                r = b.recs.get(v.key)
                if r is None:
                    r = b.recs[v.key] = Rec()
                _mx(r.r, tk, tv)
        for v in writes:
            b = v.buf
            if v.key is None:
                b.recs = {}
                b.all = Rec()
                b.all.w[tk] = tv
            else:
                r = b.recs.get(v.key)
                if r is None:
                    r = b.recs[v.key] = Rec()
                r.w = {tk: tv}
                r.r = {}

    def _wait(self, eng, need):
        for s, val in need.items():
            if eng.waited.get(s, 0) < val:
                eng.e.wait_ge(self.semh[s], val)
                eng.waited[s] = val
                self.nwaits += 1

    def op(self, eng, fn, reads, writes):
        reads = [v for v in reads if isinstance(v, V)]
        writes = [v for v in writes if isinstance(v, V)]
        need = self._need(reads, writes, eng.semkey if eng is self.PE else None)
        self._wait(eng, need)
        inst = fn()
        eng.n += 1
        inst.then_inc(eng.sem, 1)
        self._commit(reads, writes, eng.semkey, eng.n)
        self.nops += 1

    def dma(self, q, out, in_):
        reads = [in_] if isinstance(in_, V) else []
        writes = [out] if isinstance(out, V) else []
        eng = q.eng
        R = len(q.sems)
        slot = q.n % R
        prev = 16 * (q.n // R)
        need = self._need(reads, writes, None)
        if prev > 0:
            _mx(need, q.keys[slot], prev)
        self._wait(eng, need)
        oap = out.ap if isinstance(out, V) else out
        iap = in_.ap if isinstance(in_, V) else in_
        eng.e.dma_start(out=oap, in_=iap).then_inc(q.sems[slot], 16)
        q.n += 1
        self._commit(reads, writes, q.keys[slot], prev + 16)
        self.nops += 1

    def finish(self):
        need = {}
        for q in (self.qsp, self.qpool):
            R = len(q.sems)
            for i in range(min(R, q.n)):
                cnt = (q.n - 1 - i) // R + 1
                need[q.keys[i]] = 16 * cnt
        for e in (self.PE, self.ACT, self.DVE, self.POOL):
            if e.n:
                need[e.semkey] = e.n
        self._wait(self.SP, need)

    @staticmethod
    def _a(x):
        return x.ap if isinstance(x, V) else x

    def mm(self, out, lhsT, rhs, start=True, stop=True):
        nc = self.nc
        self.op(self.PE, lambda: nc.tensor.matmul(out.ap, lhsT.ap, rhs.ap, start=start, stop=stop),
                [lhsT, rhs], [out])

    def tr(self, out, in_, ident):
        nc = self.nc
        self.op(self.PE, lambda: nc.tensor.transpose(out.ap, in_.ap, ident.ap), [in_, ident], [out])

    def act(self, out, in_, func, bias=None, scale=1.0):
        nc = self.nc
        a = self._a
        kw = {}
        if bias is not None:
            kw["bias"] = a(bias)
        self.op(self.ACT, lambda: nc.scalar.activation(out=out.ap, in_=in_.ap, func=func, scale=a(scale), **kw),
                [in_, bias, scale], [out])

    def tt(self, eng, out, in0, in1, op):
        self.op(eng, lambda: eng.e.tensor_tensor(out=out.ap, in0=in0.ap, in1=in1.ap, op=op), [in0, in1], [out])

    def stt(self, out, in0, scalar, in1, op0, op1):
        nc = self.nc
        a = self._a
        self.op(self.DVE, lambda: nc.vector.scalar_tensor_tensor(out=out.ap, in0=in0.ap, scalar=a(scalar),
                                                                 in1=in1.ap, op0=op0, op1=op1),
                [in0, scalar, in1], [out])

    def ts(self, eng, out, in0, s1, s2, op0, op1=None):
        a = self._a
        if op1 is None:
            f = lambda: eng.e.tensor_scalar(out=out.ap, in0=in0.ap, scalar1=a(s1), scalar2=None, op0=op0)
        else:
            f = lambda: eng.e.tensor_scalar(out=out.ap, in0=in0.ap, scalar1=a(s1), scalar2=a(s2), op0=op0, op1=op1)
        self.op(eng, f, [in0, s1, s2], [out])

    def copy(self, eng, out, in_):
        if eng is self.ACT:
            self.act(out, in_, AF.Identity)
        else:
            self.op(eng, lambda: eng.e.tensor_copy(out=out.ap, in_=in_.ap), [in_], [out])

    def cpred(self, out, mask, data):
        nc = self.nc
        self.op(self.DVE, lambda: nc.vector.copy_predicated(out=out.ap, mask=mask.ap, data=data.ap),
                [mask, data, out], [out])

    def memset(self, eng, out, val):
        self.op(eng, lambda: eng.e.memset(out.ap, val), [], [out])

    def recip(self, out, in_):
        nc = self.nc
        self.op(self.DVE, lambda: nc.vector.reciprocal(out=out.ap, in_=in_.ap), [in_], [out])


def _chunk_consts(cl):
    s = np.arange(128)[:, None]
    t = np.arange(128)[None, :]
    same = (s // cl) == (t // cl)
    L = (same & (s <= t)).astype(np.float32)
    m = (t // cl) * cl + cl // 2 - 1
    Lm = (same & (s <= m)).astype(np.float32)
    M2 = L - Lm
    lmn = np.concatenate([L, M2, -M2], axis=1).astype(np.float32)
    U = (same & (s > t)).astype(np.float32)
    mask = (same & (t >= s)).astype(np.uint32)
    return lmn, U, mask


def build_program(nlayers=DEPTH, dbg=None):
    nc = bass.Bass("TRN2", target_bir_lowering=False)
    es = contextlib.ExitStack()

    def din(name, shape, dt=F32):
        return nc.dram_tensor(name, list(shape), dt, kind="ExternalInput").ap()

    def dout(name, shape, dt=F32):
        return nc.dram_tensor(name, list(shape), dt, kind="ExternalOutput").ap()

    xp = din("xp", [TP, D])
    xs = din("xs", [TS, D])
    sh = din("sh", [DEPTH, 16, NH, 128, 128])
    sc = din("sc", [DEPTH, 16, HALO, D])
    pp = din("pp", [DEPTH, TP, PLE])
    pps = din("pps", [DEPTH, TS, PLE])
    w_in = din("w_in", [DEPTH, D, NIN])
    w_br_a = din("w_br_a", [DEPTH, D, D])
    w_br_b = din("w_br_b", [DEPTH, D, D])
    w_o = din("w_o", [DEPTH, D, D])
    w_pg = din("w_pg", [DEPTH, D, D])
    w_ple = din("w_ple", [DEPTH, PLE, D])
    b_in = din("b_in", [DEPTH, NIN])
    lb_param = din("lb_param", [DEPTH, D])
    bfm_d = din("bfm", [128, DEPTH * 72])
    pvec_d = din("pvec", [128, DEPTH * 6 * 8])
    nf_d = din("nf", [128, 8])
    cw_d = din("cw", [128, DEPTH * 8 * CWID])
    ident_d = din("ident", [128, 128])
    lmnp_d = din("lmn_p", [128, 384])
    lmns_d = din("lmn_s", [128, 384])
    up_d = din("u_p", [128, 128])
    us_d = din("u_s", [128, 128])
    maskp_d = din("mask_p", [128, 128], U32)
    masks_d = din("mask_s", [128, 128], U32)
    bd_d = din("bd", [128, 16])
    sel_d = din("sel", [4, 512])

    yp = dout("yp", [TP, D])
    ys = dout("ys", [TS, D])
    hp = dout("hp", [DEPTH, NH, 128, 128])
    cp = dout("cp", [DEPTH, HALO, D])
    hs = dout("hs", [DEPTH, 16, NH, 128, 128])
    cs = dout("cs", [DEPTH, 16, HALO, D])

    k = KB(nc, es)
    PE, ACT, DVE, POOL = k.PE, k.ACT, k.DVE, k.POOL
    QS, QP = k.qsp, k.qpool

    Hb = k.sb("H", [128, KC, NT], F32)
    Sf = k.sb("Sf", [128, NH, 128], F32)
    Sb = k.sb("Sb", [128, NH, 128], BF16)
    ident_f = k.sb("ident_f", [128, 128], F32)
    ident_b = k.sb("ident_b", [128, 128], BF16)
    ones_b = k.sb("ones_b", [128, 128], BF16)
    LMNb = k.sb("LMNb", [128, 384], F32R)
    UMb = k.sb("UMb", [128, 128], F32R)
    MKb = k.sb("MKb", [128, 128], U32)
    B_LF = k.sb("B_LF", [128, 1024], F32R)
    bdm = k.sb("bdm", [128, 16], F32)
    bfm = k.sb("bfm", [128, 72], F32)
    hbfm = k.sb("hbfm", [128, 24], F32)
    pvec = k.sb("pvec", [128, 6 * 8], F32)
    nfv = k.sb("nfv", [128, 8], F32)
    cwv = k.sb("cwv", [128, 8 * CWID], F32)
    OMLB = k.sb("OMLB", [128, D], F32)
    brow = k.sb("brow", [4, 512], BF16)
    selb = k.sb("selb", [4, 4, 128], BF16)

    B_XN = k.sb("B_XN", [128, 4096], BF16)
    B_VS = k.sb("B_VS", [128, 8192], BF16)
    B_SQ = k.sb("B_SQ", [128, 4096], BF16)
    B_OG = k.sb("B_OG", [128, 4096], BF16)
    B_F = k.sb("B_F", [128, 2048], F32)
    B_MISC = k.sb("B_MISC", [128, 1024], F32)
    B_KK = k.sb("B_KK", [128, 1024], BF16)
    B_EQ = k.sb("B_EQ", [128, 3072], F32)
    B_OSB = k.sb("B_OSB", [128, 1024], F32)
    B_OSQ = k.sb("B_OSQ", [128, 1024], BF16)
    B_DC = k.sb("B_DC", [128, NH * 16], F32)
    B_AM = k.sb("B_AM", [128, 8, 128], BF16)
    B_RS = k.sb("B_RS", [128, 512], F32)
    B_U = k.sb("B_U", [128, KC * 608], BF16)
    NRING = 4
    WR = [k.sb(f"WR{i}", [128, KC, 512], BF16) for i in range(NRING)]

    PB = [k.ps(f"PB{i}", [128, 512]) for i in range(8)]

    dbg_out = {}
    if dbg:
        for name, shape in dbg.items():
            dbg_out[name] = dout("dbg_" + name, shape[0], shape[1])

    def dump(name, v):
        if name in dbg_out:
            k.dma(QS, dbg_out[name], v)

    class WS:
        def __init__(self):
            self.sched = []
            self.issued = 0
            self.used = 0

        def issue(self, i):
            src, col0, kc, tag, lyr = self.sched[i]
            b = WR[i % NRING]
            if lyr >= 1:
                sb_, sap_ = scr[(tag[0], lyr)]
                ap = V(sb_, None, sap_[:, col0:col0 + 512].rearrange("(kc p) n -> p kc n", p=128))
            else:
                ap = src[:, col0:col0 + 512].rearrange("(kc p) n -> p kc n", p=128)
            k.dma(QP, b.v(b.t[:, 0:kc, :]), ap)

        def get(self, src_name, col0):
            assert self.used < len(self.sched), "weight schedule exhausted"
            s = self.sched[self.used]
            assert s[3] == (src_name, col0), (s[3], src_name, col0)
            while self.issued < min(len(self.sched), self.used + NRING - 1):
                self.issue(self.issued)
                self.issued += 1
            b = WR[self.used % NRING]
            self.used += 1
            return b

    ws = WS()
    k.marks = []
    WSRC = {"w_in": (w_in, D, NIN), "w_br_a": (w_br_a, D, D), "w_br_b": (w_br_b, D, D), "w_o": (w_o, D, D),
            "w_pg": (w_pg, D, D), "w_ple": (w_ple, PLE, D)}
    scr = {}
    for name_, (src_, rows_, cols_) in WSRC.items():
        for l_ in range(1, nlayers):
            t_ = nc.dram_tensor(f"scr_{name_}_{l_}", [rows_, cols_], BF16, kind="Internal")
            scr[(name_, l_)] = (Buf(f"scr_{name_}_{l_}", t_), t_.ap())

    def precast(ln, q):
        for name_, (src_, rows_, cols_) in WSRC.items():
            b_, ap_ = scr[(name_, ln)]
            r_ = rows_ // 4
            nsub = 4 if name_ == "w_in" else 1
            rs_ = r_ // nsub
            for u_ in range(nsub):
                r0_ = q * r_ + u_ * rs_
                k.dma(QP, V(b_, (q, u_), ap_[r0_:r0_ + rs_, :]), src_[ln, r0_:r0_ + rs_, :])


    def mark(name):
        k.marks.append((name, k.PE.n, k.ACT.n, k.DVE.n, k.POOL.n))

    tiles = [(0, 512, "p"), (512, 512, "p"), (1024, 512, "p"), (1536, 512, "p"), (TP, TS, "s")]
    for l in range(nlayers):
        for _ in tiles:
            def add(src, name, col0, kc=KC):
                ws.sched.append((src[l], col0, kc, (name, col0), l))
            for nm in ("iv", "q", "za", "f", "gb", "zb", "ga"):
                add(w_in, "w_in", OFF[nm])
                add(w_in, "w_in", OFF[nm] + 512)
            for half in range(2):
                add(w_in, "w_in", OFF["ma"] + half * 512)
                add(w_in, "w_in", OFF["mb"] + half * 512)
            for half in range(2):
                add(w_br_a, "w_br_a", half * 512)
                add(w_br_b, "w_br_b", half * 512)
            add(w_o, "w_o", 0)
            add(w_o, "w_o", 512)
            add(w_pg, "w_pg", 0)
            add(w_pg, "w_pg", 512)
            add(w_ple, "w_ple", 0, 2)
            add(w_ple, "w_ple", 512, 2)

    for dst, src in ((ident_f, ident_d), (bdm, bd_d), (nfv, nf_d)):
        k.dma(QS, dst[:], src)

    def load_consts(typ, part="both"):
        lm_d, u_d, m_d = (lmnp_d, up_d, maskp_d) if typ == "p" else (lmns_d, us_d, masks_d)
        if part in ("both", "dma"):
            k.dma(QS, B_MISC.v(B_MISC.t[:, 0:384]), lm_d)
            k.dma(QS, B_MISC.v(B_MISC.t[:, 384:512]), u_d)
            k.dma(QS, MKb[:], m_d)
        if part in ("both", "copy"):
            k.copy(DVE, LMNb[:], B_MISC.v(B_MISC.t[:, 0:384]))
            k.copy(DVE, UMb[:], B_MISC.v(B_MISC.t[:, 384:512]))

    k.copy(DVE, ident_b[:], ident_f[:])
    k.dma(QP, selb.v(selb.t[:, :, :].rearrange('p a b -> p (a b)')), sel_d)
    k.memset(DVE, ones_b[:], 1.0)
    k.memset(DVE, B_AM[:], 0.0)

    def pv(l, name, c):
        i = PV[name] * 8 + c
        return pvec[:, i:i + 1]

    def bcol(l, col):
        return bfm[:, col:col + 1]

    def hbcol(l, col):
        nm_ = {OFF['gb'] // 128: 0, OFF['ma'] // 128: 1, OFF['mb'] // 128: 2}[(col // 8) * 8]
        i = nm_ * 8 + col % 8
        return hbfm[:, i:i + 1]

    def tv(buf, TT, off=0):
        return buf.t[:, off:off + KC * TT].rearrange("p (c t) -> p c t", c=KC)

    def rms_rstd(t0, TT, tkey, rsbuf, rsap=None):
        hsq = tv(B_SQ, TT)
        k.act(B_SQ.v(hsq), Hb.k(tkey, (slice(None), slice(None), slice(t0, t0 + TT))), AF.Square)
        ps = PB[0]
        for c in range(KC):
            k.mm(ps.v(ps.t[:, 0:TT]), ones_b[:], B_SQ.v(hsq[:, c, :]), start=(c == 0), stop=(c == KC - 1))
        if rsap is None:
            rsap = rsbuf.t[:, 0:TT]
        k.act(rsbuf.v(rsap), ps.v(ps.t[:, 0:TT]), AF.Ln, bias=EPS, scale=1.0 / D)
        k.act(rsbuf.v(rsap), rsbuf.v(rsap), AF.Exp, scale=-0.5)

    TILEBUFS = (B_XN, B_VS, B_SQ, B_OG, B_F, B_MISC, B_EQ, B_OSB, B_OSQ, B_U, B_KK)

    def merge_all():
        for b in TILEBUFS:
            b.merge()

    omlb_t = nc.dram_tensor("scr_omlb", [DEPTH, 128, D], F32, kind="Internal")
    omlb_buf, omlb_ap = Buf("scr_omlb", omlb_t), omlb_t.ap()
    LBT = B_VS.t[:, 0:8192].bitcast(F32)
    k.dma(QS, B_VS.v(LBT), lb_param.rearrange("l d -> (l d)").partition_broadcast(128))
    k.act(B_VS.v(LBT), B_VS.v(LBT), AF.Exp)
    ssum = B_OSB.t[:, 0:1024]
    num = B_MISC.t[:, 0:1024]
    k.tt(DVE, B_OSB.v(ssum), B_VS.v(LBT[:, 0:1024]), B_VS.v(LBT[:, 1024:2048]), ALU.add)
    k.tt(DVE, B_OSB.v(ssum), B_OSB.v(ssum), B_VS.v(LBT[:, 2048:3072]), ALU.add)
    k.tt(DVE, B_OSB.v(ssum), B_OSB.v(ssum), B_VS.v(LBT[:, 3072:4096]), ALU.add)
    k.recip(B_OSB.v(ssum), B_OSB.v(ssum))
    for l_ in range(nlayers):
        k.copy(DVE, B_MISC.v(num), B_VS.v(LBT[:, 0:1024]))
        for j in range(l_ + 1, DEPTH):
            k.tt(DVE, B_MISC.v(num), B_MISC.v(num), B_VS.v(LBT[:, j * 1024:(j + 1) * 1024]), ALU.add)
        k.tt(DVE, OMLB[:], B_MISC.v(num), B_OSB.v(ssum), ALU.mult)
        k.dma(QS, V(omlb_buf, l_, omlb_ap[l_]), OMLB[:])

    XST = B_MISC
    for blk in range(NT // 128):
        src = xp[blk * 128:(blk + 1) * 128, :] if blk < 16 else xs
        k.dma(QS, XST[:], src)
        for half in range(2):
            ps = PB[half]
            for j in range(4):
                c = half * 4 + j
                k.tr(ps.v(ps.t[:, j * 128:(j + 1) * 128]), XST.v(XST.t[:, c * 128:(c + 1) * 128]), ident_f[:])
            tkey = min(blk // 4, 4)
            k.copy(ACT if half else DVE,
                   Hb.k(tkey, (slice(None), slice(half * 4, half * 4 + 4), slice(blk * 128, (blk + 1) * 128))),
                   ps.v(ps.t[:, :].rearrange("p (c t) -> p c t", c=4)))

    KKF = B_KK.t[:, 0:1024].bitcast(F32)
    rms_rstd(0, 512, 0, B_KK, KKF[:, 0:512])
    for l in range(nlayers):
        k.dma(QS, OMLB[:], V(omlb_buf, l, omlb_ap[l]))
        k.dma(QP, brow[:], b_in[l, 1024:3072].rearrange('(r n) -> r n', r=4))
        k.dma(QS, cwv[:], cw_d[:, l * 8 * CWID:(l + 1) * 8 * CWID])
        k.dma(QS, bfm[:], bfm_d[:, l * 72:(l + 1) * 72])
        k.dma(QS, pvec[:], pvec_d[:, l * 48:(l + 1) * 48])
        load_consts("p", "dma")
        k.memset(POOL, Sf[:], 0.0)
        k.memset(POOL, Sb[:], 0.0)
        k.memset(POOL, B_U[:], 0.0)

        def deferred_layer_setup():
            for i_, nm_ in enumerate(('gb', 'ma', 'mb')):
                c0_ = OFF[nm_] // 128
                k.ts(DVE, hbfm.v(hbfm.t[:, i_ * 8:i_ * 8 + 8]), bfm.v(bfm.t[:, c0_:c0_ + 8]), 0.5, None, ALU.mult)
            load_consts("p", "copy")
            k.memset(DVE, B_AM[:], 0.0)

        for ti, (t0, TT, typ) in enumerate(tiles):
            NB = TT // 128
            tkey = ti
            CL = 64 if typ == "p" else 8
            NCH = 128 // CL
            merge_all()
            if typ == "s":
                load_consts("s")
                k.memset(DVE, B_AM[:], 0.0)
            XN = tv(B_XN, TT)
            VTM = B_VS.t[:, 0:NB * 1024].rearrange("p (b n) -> p b n", b=NB)
            SZA = tv(B_VS, TT, off=NB * 1024)
            SQ = tv(B_SQ, TT)
            OG = tv(B_OG, TT)
            Hsl = (slice(None), slice(None), slice(t0, t0 + TT))

            mark(f'L{l}T{ti} start')
            for c in range(KC):
                k.stt(B_XN.v(XN[:, c, :]), Hb.k(tkey, (slice(None), c, slice(t0, t0 + TT))), pv(l, "norm_mix", c),
                      B_KK.v(KKF[:, 0:TT]), ALU.mult, ALU.mult)
            if ti == 0:
                deferred_layer_setup()

            mark(f'L{l}T{ti} iv')
            pj = 0
            for half in range(2):
                wg = ws.get("w_in", OFF["iv"] + half * 512)
                for blk in range(NB):
                    ps = PB[pj % 2]
                    pj += 1
                    k.mm(ps[:], selb[0:4, 2 + half, :], brow[0:4, :], start=True, stop=False)
                    for c in range(KC):
                        k.mm(ps[:], B_XN.v(XN[:, c, blk * 128:(blk + 1) * 128]), wg[:, c, :], start=False, stop=(c == KC - 1))
                    k.copy(ACT, B_VS.v(VTM[:, blk, half * 512:(half + 1) * 512], 'vtm'), ps[:])

            for nm, dstb, dst, dkey in (("q", B_SQ, SQ, None), ("za", B_VS, SZA, "sza")):
                for half in range(2):
                    wg = ws.get("w_in", OFF[nm] + half * 512)
                    for j in range(4):
                        hd = half * 4 + j
                        ps = PB[pj % 2]
                        pj += 1
                        for c in range(KC):
                            k.mm(ps.v(ps.t[:, 0:TT]), wg[:, c, j * 128:(j + 1) * 128], B_XN.v(XN[:, c, :]),
                                 start=(c == 0), stop=(c == KC - 1))
                        k.act(dstb.v(dst[:, hd, :], dkey), ps.v(ps.t[:, 0:TT]), AF.Silu, bias=bcol(l, OFF[nm] // 128 + hd))

            mark(f'L{l}T{ti} hgrn')
            if typ == 'p' and l + 1 < nlayers:
                precast(l + 1, ti)
            wf = [ws.get("w_in", OFF["f"]), ws.get("w_in", OFF["f"] + 512)]
            LFR = B_LF.t[:, 0:1024]
            LF = LFR
            E4 = B_MISC.t[:, 0:1024]
            KK = B_KK.t[:, 0:1024]
            QBQ = B_EQ.t[:, 1536:3072].bitcast(BF16).rearrange("p (h j t) -> p h j t", h=NH, j=3)
            OSB = B_OSB.t[:, 0:1024].rearrange("p (h t) -> p h t", h=NH)
            OSQ = B_OSQ.t[:, 0:1024].rearrange("p (h t) -> p h t", h=NH)
            DC = B_DC.t[:, :].rearrange("p (h c) -> p h c", h=NH)
            LMN = LMNb
            UM = UMb
            MK = MKb
            THKs = [B_F.t[:, 0:1024], B_F.t[:, 1024:2048]]

            def f_pre_a(blk):
                nonlocal pj
                bs_ = slice(blk * 128, (blk + 1) * 128)
                thk_ = THKs[blk % 2]
                tk_ = "thk%d" % (blk % 2)
                for half in range(2):
                    ps = PB[pj % 2]
                    pj += 1
                    k.mm(ps[:], selb[0:4, half, :], brow[0:4, :], start=True, stop=False)
                    for c in range(KC):
                        k.mm(ps[:], B_XN.v(XN[:, c, bs_]), wf[half][:, c, :], start=False, stop=(c == KC - 1))
                    th = B_F.v(thk_[:, half * 512:(half + 1) * 512], tk_)
                    k.act(th, ps[:], AF.Exp)
                    k.act(th, th, AF.Ln, bias=1.0)
                    k.act(th, th, AF.Exp, scale=-1.0)

            def f_pre_b(blk):
                thk_ = THKs[blk % 2]
                tk_ = "thk%d" % (blk % 2)
                k.tt(DVE, B_F.v(thk_, tk_), B_F.v(thk_, tk_), OMLB[:], ALU.mult)

            f_pre_a(0)
            f_pre_b(0)
            for blk in range(NB):
                bs = slice(blk * 128, (blk + 1) * 128)
                THK = THKs[blk % 2]
                TKEY = "thk%d" % (blk % 2)
                k.act(B_LF.v(LFR, "lf"), B_F.v(THK, TKEY), AF.Ln, bias=1.0, scale=-1.0)
                for half in range(2):
                    ps = PB[half]
                    k.mm(ps[:], UM[:], B_LF.v(LFR[:, half * 512:(half + 1) * 512], "lf"))
                    k.act(B_MISC.v(E4[:, half * 512:(half + 1) * 512]), ps[:], AF.Exp)
                k.tt(DVE, B_KK.v(KK), B_F.v(THK, TKEY), B_MISC.v(E4), ALU.mult)
                if l == 0 and typ == 's':
                    dump('ktm', B_F.v(THK, TKEY))
                    dump('lf', B_LF.v(LF, 'lf'))
                    dump('e4', B_MISC.v(E4))
                    dump('vtm', B_VS.v(VTM[:, 0, :]))
                    dump('xn', B_XN.v(XN))
                for g in range(4):
                    Eg = B_EQ.t[:, (g % 2) * 768:(g % 2 + 1) * 768].rearrange("p (h n) -> p h n", h=2)
                    for j in range(2):
                        hd = 2 * g + j
                        ekey = "E%d_%d" % (g % 2, j)
                        ps = PB[2 + hd % 4]
                        k.mm(ps.v(ps.t[:, 0:384]), B_LF.v(LFR[:, hd * 128:(hd + 1) * 128], "lf"), LMN[:])
                        k.tr(ps.v(ps.t[:, 384:512]), B_F.v(THK[:, hd * 128:(hd + 1) * 128], TKEY), ident_f[:])
                        k.act(B_EQ.v(Eg[:, j, :], ekey), ps.v(ps.t[:, 0:384]), AF.Exp)
                        k.tt(DVE, B_EQ.v(QBQ[:, hd, 0:2, :], "qbq"),
                             B_SQ.v(SQ[:, hd, bs].unsqueeze(1).broadcast_to([128, 2, 128])),
                             B_EQ.v(Eg[:, j, 0:256].rearrange("p (j t) -> p j t", j=2), ekey), ALU.mult)
                        k.tt(DVE, B_EQ.v(QBQ[:, hd, 2, :], "qbq"), ps.v(ps.t[:, 384:512]),
                             B_EQ.v(Eg[:, j, 256:384], ekey), ALU.mult)
                        k.copy(POOL, B_DC.v(DC[:, hd, 0:NCH]),
                               B_EQ.v(Eg[:, j, 0:128].rearrange("p (c t) -> p c t", t=CL)[:, :, CL - 1], ekey))

                if blk + 1 < NB:
                    f_pre_a(blk + 1)
                if l == 0 and typ == 's':
                    dump('dc', B_DC[:])
                    dump('qbq', B_EQ.v(QBQ))
                def oreg(hd, lo, hi):
                    po = PB[6 + hd // 4]
                    return po.v(po.t[:, (hd % 4) * 128 + lo:(hd % 4) * 128 + hi])

                def sfv(hd):
                    return Sf.k(hd, (slice(None), hd, slice(None)))

                def sbv(hd):
                    return Sb.k(hd, (slice(None), hd, slice(None)))

                for hd in range(NH):
                    pa = PB[4 + hd % 2]
                    k.mm(pa.v(pa.t[:, 0:128]), B_EQ.v(QBQ[:, hd, 2, :], "qbq"), B_EQ.v(QBQ[:, hd, 1, :], "qbq"))
                    k.cpred(B_AM.k(hd, (slice(None), hd, slice(None))), MK[:], pa.v(pa.t[:, 0:128]))
                if typ == "p":
                    for c in range(NCH):
                        cs_ = slice(c * CL, (c + 1) * CL)
                        for hd in range(NH):
                            hs_ = slice(hd * 128, (hd + 1) * 128)
                            ov = oreg(hd, c * CL, (c + 1) * CL)
                            k.mm(ov, sbv(hd), B_EQ.v(QBQ[:, hd, 0, cs_], "qbq"), start=True, stop=False)
                            k.mm(ov, B_VS.v(VTM[cs_, blk, hs_], 'vtm'), B_AM.k(hd, (cs_, hd, cs_)), start=False, stop=True)
                            pa = PB[4 + hd % 2]
                            sv = pa.v(pa.t[:, 0:128])
                            k.mm(sv, B_KK.v(KK[cs_, hs_]), B_VS.v(VTM[cs_, blk, hs_], 'vtm'))
                            k.stt(sfv(hd), sfv(hd), B_DC.v(DC[:, hd, c:c + 1]), sv, ALU.mult, ALU.add)
                            k.copy(POOL, sbv(hd), sfv(hd))
                else:
                    S0fB = [B_VS, B_U]
                    S0f = [B_VS.t[:, 2048:6144].bitcast(F32).rearrange("p (s v) -> p s v", s=16),
                           B_U.t[:, 0:4096].bitcast(F32).rearrange("p (s v) -> p s v", s=16)]
                    S0bB = [B_VS, B_F]
                    S0b = [B_VS.t[:, 6144:8192].rearrange("p (s v) -> p s v", s=16),
                           B_F.t[:, 1024:2048].bitcast(BF16).rearrange("p (s v) -> p s v", s=16)]
                    VBDs = [(B_XN, B_XN.t[:, 1024:3072].rearrange("p (s v) -> p s v", s=16)),
                            (B_EQ, B_EQ.t[:, 0:1024].bitcast(BF16).rearrange("p (s v) -> p s v", s=16))]
                    SOUT = [B_SQ.t[:, 1024:3072].bitcast(F32).rearrange("p (s v) -> p s v", s=8),
                            B_OG.t[:, 1024:3072].bitcast(F32).rearrange("p (s v) -> p s v", s=8)]
                    SOB = [B_SQ, B_OG]

                    def load_state(hd):
                        i = hd % 2
                        k.dma(QP, S0fB[i].v(S0f[i], "s0f"), sh[l, :, hd].rearrange("s k v -> k s v"))

                    def cast_state(hd):
                        i = hd % 2
                        k.copy(ACT, S0bB[i].v(S0b[i], "s0b"), S0fB[i].v(S0f[i], "s0f"))

                    def build_vbd(hd):
                        vb_, vbd_ = VBDs[hd % 2]
                        hsl_ = slice(hd * 128, (hd + 1) * 128)
                        k.tt(POOL, vb_.v(vbd_, "vbd"), B_VS.v(VTM[:, 0, hsl_].unsqueeze(1).broadcast_to([128, 16, 128]), 'vtm'),
                             bdm.v(bdm.t[:, :].unsqueeze(2).broadcast_to([128, 16, 128])), ALU.mult)

                    B_EQ.merge()
                    load_state(0)
                    cast_state(0)
                    build_vbd(0)
                    for hd in range(NH):
                        hs_ = slice(hd * 128, (hd + 1) * 128)
                        i = hd % 2
                        VBDb, VBD = VBDs[hd % 2]
                        if hd + 1 < NH:
                            load_state(hd + 1)
                            build_vbd(hd + 1)
                        k.mm(oreg(hd, 0, 128), B_VS.v(VTM[:, 0, hs_], 'vtm'), B_AM.k(hd, (slice(None), hd, slice(None))),
                             start=True, stop=False)
                        for s_ in range(16):
                            k.mm(oreg(hd, s_ * 8, (s_ + 1) * 8), S0bB[i].v(S0b[i][:, s_, :], "s0b"),
                                 B_EQ.v(QBQ[:, hd, 0, s_ * 8:(s_ + 1) * 8], "qbq"), start=False, stop=(s_ == 15))
                        for hf in range(2):
                            so = SOUT[hf]
                            sob = SOB[hf]
                            for qd in range(2):
                                q4 = hf * 2 + qd
                                ps = PB[q4]
                                k.mm(ps[:], B_KK.v(KK[:, hs_]),
                                     VBDb.v(VBD[:, q4 * 4:(q4 + 1) * 4, :], "vbd"))
                                for s4 in range(4):
                                    s_ = q4 * 4 + s4
                                    k.stt(sob.v(so[:, qd * 4 + s4, :], "sout"), S0fB[i].v(S0f[i][:, s_, :], "s0f"),
                                          B_DC.v(DC[:, hd, s_:s_ + 1]), ps.v(ps.t[:, s4 * 128:(s4 + 1) * 128]),
                                          ALU.mult, ALU.add)
                            k.dma(QS, hs[l, hf * 8:(hf + 1) * 8, hd].rearrange("s k v -> k s v"), sob.v(so, "sout"))
                        if hd + 1 < NH:
                            cast_state(hd + 1)
                if blk + 1 < NB:
                    f_pre_b(blk + 1)
                for hb in range(2):
                    po = PB[6 + hb]
                    k.act(B_OSB.v(B_OSB.t[:, hb * 512:(hb + 1) * 512]), po[:], AF.Identity)
                    k.act(B_OSQ.v(B_OSQ.t[:, hb * 512:(hb + 1) * 512]), po[:], AF.Square)

                RSO = B_MISC.t[:, 0:1024]
                for half in range(2):
                    ps = PB[half]
                    k.mm(ps[:], ones_b[:], B_OSQ.v(B_OSQ.t[:, half * 512:(half + 1) * 512]))
                    rs = B_MISC.v(RSO[:, half * 512:(half + 1) * 512])
                    k.act(rs, ps[:], AF.Ln, bias=EPS, scale=1.0 / 128)
                    k.act(rs, rs, AF.Exp, scale=-0.5)
                k.tt(DVE, B_OSB.v(B_OSB.t[:, 0:1024]), B_OSB.v(B_OSB.t[:, 0:1024]), B_MISC.v(RSO), ALU.mult)
                for hd in range(NH):
                    k.stt(B_OG.v(OG[:, hd, bs]), B_OSB.v(OSB[:, hd, :]), pv(l, "gnorm_a", hd), B_VS.v(SZA[:, hd, bs], 'sza'),
                          ALU.mult, ALU.mult)

            if typ == "p" and ti == 3:
                k.dma(QS, hp[l].rearrange("h k v -> k h v"), Sf[:])

            mark(f'L{l}T{ti} conv')
            merge_all()
            if typ == "p":
                UW = 30 + TT
                Uv = B_U.t[:, 0:KC * UW].rearrange("p (c t) -> p c t", c=KC)
            else:
                Uv = B_U.t[:, 0:KC * 608].rearrange("p (c s t) -> p c s t", c=KC, s=16)
                for q4 in range(4):
                    CST = B_MISC
                    k.dma(QS, CST.v(CST.t[0:120, 0:1024]), sc[l, q4 * 4:(q4 + 1) * 4].rearrange("s t d -> (s t) d"))
                    for half in range(2):
                        ps = PB[half]
                        for j in range(4):
                            cb = half * 4 + j
                            k.tr(ps.v(ps.t[:, j * 128:j * 128 + 120]), CST.v(CST.t[0:120, cb * 128:(cb + 1) * 128]),
                                 ident_f[0:120, 0:120])
                        k.copy(DVE, B_U.v(Uv[:, half * 4:half * 4 + 4, q4 * 4:(q4 + 1) * 4, 0:30]),
                               ps.v(ps.t[:, :].rearrange("p (c x) -> p c x", c=4)[:, :, 0:120]
                                    .rearrange("p c (s t) -> p c s t", s=4)))
                k.dma(QS, cs[l, :, 0:22, :], sc[l, :, 8:30, :])
            CV = tv(B_VS, TT)
            SIG = tv(B_VS, TT, off=4096)
            SZB = tv(B_SQ, TT)
            for nm in ("gb", "zb"):
                for half in range(2):
                    wg = ws.get("w_in", OFF[nm] + half * 512)
                    for j in range(4):
                        cb = half * 4 + j
                        ps = PB[pj % 2]
                        pj += 1
                        for c in range(KC):
                            k.mm(ps.v(ps.t[:, 0:TT]), wg[:, c, j * 128:(j + 1) * 128], B_XN.v(XN[:, c, :]),
                                 start=(c == 0), stop=(c == KC - 1))
                        if nm == "gb":
                            k.act(B_VS.v(SIG[:, cb, :], "sig"), ps.v(ps.t[:, 0:TT]), AF.Tanh, bias=hbcol(l, OFF[nm] // 128 + cb), scale=0.5)
                            k.ts(DVE, B_VS.v(SIG[:, cb, :], "sig"), B_VS.v(SIG[:, cb, :], "sig"), 0.5, 0.5, ALU.mult, ALU.add)
                        else:
                            k.act(B_SQ.v(SZB[:, cb, :]), ps.v(ps.t[:, 0:TT]), AF.Silu, bias=bcol(l, OFF[nm] // 128 + cb))
            Dgs = [(B_F, B_F.t[:, 0:CWID * 64].bitcast(BF16).rearrange("p (j c) -> p j c", j=CWID)),
                   (B_EQ, B_EQ.t[:, 0:CWID * 64].bitcast(BF16).rearrange("p (j c) -> p j c", j=CWID))]

            def build_d(cb):
                db, dg = Dgs[cb % 2]
                cw0 = cb * CWID
                k.tt(DVE, db.v(dg), ident_b.v(ident_b.t[:, :].unsqueeze(1).broadcast_to([128, CWID, 128])),
                     cwv.v(cwv.t[:, cw0:cw0 + CWID].unsqueeze(2).broadcast_to([128, CWID, 128])), ALU.mult)

            build_d(0)
            B_UF = B_MISC
            UF = B_MISC.t[:, 0:256].rearrange('p (c t) -> p c t', c=KC)
            s1 = PB[4]
            s2 = PB[5]
            gaw = {}

            def ga_u(cb):
                nonlocal pj
                half, j = cb // 4, cb % 4
                if j == 0:
                    gaw[half] = ws.get("w_in", OFF["ga"] + half * 512)
                wg = gaw[half]
                ps = PB[pj % 2]
                pj += 1
                for c in range(KC):
                    k.mm(ps.v(ps.t[:, 0:TT]), wg[:, c, j * 128:(j + 1) * 128], B_XN.v(XN[:, c, :]),
                         start=(c == 0), stop=(c == KC - 1))
                gab = bcol(l, OFF["ga"] // 128 + cb)
                if typ == "p":
                    udst = Uv[:, cb, 30:30 + TT]
                    k.stt(B_U.v(udst, cb), ps.v(ps.t[:, 0:TT]), gab, B_VS.v(SIG[:, cb, :], "sig"), ALU.add, ALU.mult)
                    if ti == 3:
                        k.stt(B_UF.v(UF[:, cb, :]), ps.v(ps.t[:, TT - 32:TT]), gab, B_VS.v(SIG[:, cb, TT - 32:TT], "sig"),
                              ALU.add, ALU.mult)
                else:
                    udst = Uv[:, cb, :, 30:38]
                    k.stt(B_U.v(udst, cb), ps.v(ps.t[:, 0:TT].rearrange("p (s t) -> p s t", s=16)), gab,
                          B_VS.v(SIG[:, cb, :].rearrange("p (s t) -> p s t", s=16), "sig"), ALU.add, ALU.mult)
                    k.stt(B_OSB.v(B_OSB.t[:, cb * 128:(cb + 1) * 128]), ps.v(ps.t[:, 0:TT]), gab,
                          B_VS.v(SIG[:, cb, :], "sig"), ALU.add, ALU.mult)

            def conv_cb(cb):
                if cb + 1 < KC:
                    build_d(cb + 1)
                DgB, Dg = Dgs[cb % 2]
                pc = PB[2 + cb % 2]
                for tap in range(CWID):
                    if typ == "p":
                        rhs = Uv[:, cb, tap:tap + TT]
                    else:
                        rhs = Uv[:, cb, :, tap:tap + 8]
                    k.mm(pc.v(pc.t[:, 0:TT]), DgB.v(Dg[:, tap, :]), B_U.v(rhs, cb), start=(tap == 0), stop=(tap == CWID - 1))
                k.act(B_VS.v(CV[:, cb, :], "cv%d" % cb), pc.v(pc.t[:, 0:TT]), AF.Identity, bias=pv(l, "conv_b", cb))
                csq = B_OSQ.t[:, (cb % 2) * 512:(cb % 2) * 512 + TT]
                k.act(B_OSQ.v(csq, "csq%d" % (cb % 2)), pc.v(pc.t[:, 0:TT]), AF.Square, bias=pv(l, "conv_b", cb))
                if cb > 0:
                    conv_stats(cb - 1)

            def conv_stats(cb):
                csq = B_OSQ.t[:, (cb % 2) * 512:(cb % 2) * 512 + TT]
                k.mm(s1.v(s1.t[:, 0:TT]), ones_b[:], B_VS.v(CV[:, cb, :], "cv%d" % cb), start=(cb == 0), stop=(cb == KC - 1))
                k.mm(s2.v(s2.t[:, 0:TT]), ones_b[:], B_OSQ.v(csq, "csq%d" % (cb % 2)), start=(cb == 0), stop=(cb == KC - 1))

            ga_u(0)
            for cb in range(KC):
                if cb + 1 < KC:
                    ga_u(cb + 1)
                conv_cb(cb)
            conv_stats(KC - 1)
            if typ == "p":
                k.copy(ACT, B_U.v(Uv[:, :, 0:30]), B_U.v(Uv[:, :, TT:TT + 30]))
                if ti == 3:
                    ps = PB[0]
                    for cb in range(KC):
                        pst = PB[cb // 4]
                        k.tr(pst.v(pst.t[0:32, (cb % 4) * 128:(cb % 4 + 1) * 128]), B_UF.v(UF[:, cb, :]), ident_f[:])
                    for half in range(2):
                        pst = PB[half]
                        k.copy(DVE, B_MISC.v(B_MISC.t[0:32, half * 512:(half + 1) * 512]), pst.v(pst.t[0:32, :]))
                    k.dma(QS, cp[l], B_MISC.v(B_MISC.t[2:32, 0:1024]))
            else:
                for cb in range(KC):
                    pst = PB[cb // 4]
                    k.tr(pst.v(pst.t[:, (cb % 4) * 128:(cb % 4 + 1) * 128]), B_OSB.v(B_OSB.t[:, cb * 128:(cb + 1) * 128]), ident_f[:])
                for half in range(2):
                    pst = PB[half]
                    k.copy(DVE, B_MISC.v(B_MISC.t[:, half * 512:(half + 1) * 512]), pst[:])
                for sq_ in range(16):
                    k.dma(QS, cs[l, sq_, 22:30, :], B_MISC.v(B_MISC.t[sq_ * 8:(sq_ + 1) * 8, 0:1024]))
            mark(f'L{l}T{ti} ln')
            merge_all()
            psrc = pp[l] if typ == "p" else pps[l]
            pr0 = t0 if typ == "p" else 0
            k.dma(QS, B_MISC.v(B_MISC.t[:, 0:NB * PLE].rearrange("p (b d) -> p b d", b=NB)),
                  psrc[pr0:pr0 + TT, :].rearrange("(b p) d -> p b d", p=128))
            B_RS2 = B_KK
            MU = B_KK.t[:, 0:1024].bitcast(F32)[:, 0:TT]
            RS = B_RS.t[:, 0:TT]
            k.ts(DVE, B_RS2.v(MU), s1.v(s1.t[:, 0:TT]), 1.0 / D, None, ALU.mult)
            k.tt(DVE, B_RS.v(RS), B_RS2.v(MU), B_RS2.v(MU), ALU.mult)
            k.stt(B_RS.v(RS), s2.v(s2.t[:, 0:TT]), 1.0 / D, B_RS.v(RS), ALU.mult, ALU.subtract)
            k.act(B_RS.v(RS), B_RS.v(RS), AF.Ln, bias=EPS)
            k.act(B_RS.v(RS), B_RS.v(RS), AF.Exp, scale=-0.5)
            CVG = B_EQ.t[:, 0:KC * TT // 2].bitcast(BF16).rearrange("p (c t) -> p c t", c=KC)
            XH = B_OSB.t[:, 0:TT]
            TMbuf = {"ma": B_F, "mb": B_VS}
            TM = {"ma": B_F.t[:, 0:KC * TT // 2].bitcast(BF16).rearrange("p (c t) -> p c t", c=KC), "mb": tv(B_VS, TT, off=4096)}
            wgm = {}
            for i_ in range(KC):
                cb = i_
                k.tt(DVE, B_OSB.v(XH), B_VS.v(CV[:, cb, :], "cv"), B_RS2.v(MU), ALU.subtract)
                k.tt(DVE, B_OSB.v(XH), B_OSB.v(XH), B_RS.v(RS), ALU.mult)
                k.act(B_VS.v(CV[:, cb, :], "cv"), B_OSB.v(XH), AF.Silu, bias=pv(l, "ln_b", cb), scale=pv(l, "ln_g", cb))
                k.tt(DVE, B_EQ.v(CVG[:, cb, :]), B_VS.v(CV[:, cb, :], "cv"), B_SQ.v(SZB[:, cb, :]), ALU.mult)
                ob = i_
                half, j = ob // 4, ob % 4
                if j == 0:
                    wgm["ma"] = ws.get("w_in", OFF["ma"] + half * 512)
                    wgm["mb"] = ws.get("w_in", OFF["mb"] + half * 512)
                for nm in ("ma", "mb"):
                    wg = wgm[nm]
                    ps = PB[pj % 2]
                    pj += 1
                    for c in range(KC):
                        k.mm(ps.v(ps.t[:, 0:TT]), wg[:, c, j * 128:(j + 1) * 128], B_XN.v(XN[:, c, :]),
                             start=(c == 0), stop=(c == KC - 1))
                    k.act(TMbuf[nm].v(TM[nm][:, ob, :], nm), ps.v(ps.t[:, 0:TT]), AF.Tanh, bias=hbcol(l, OFF[nm] // 128 + ob), scale=0.5)

            mark(f'L{l}T{ti} merge')
            PT = B_OSQ.t[:, 0:1024].rearrange('p (c t) -> p c t', c=2)
            for blk in range(NB):
                ps = PB[2 + blk % 2]
                for c in range(2):
                    k.tr(ps.v(ps.t[:, c * 128:(c + 1) * 128]), B_MISC.v(B_MISC.t[:, blk * PLE + c * 128:blk * PLE + (c + 1) * 128]), ident_f[:])
                k.copy(DVE, B_OSQ.v(PT[:, :, blk * 128:(blk + 1) * 128]),
                       ps.v(ps.t[:, 0:256].rearrange("p (c t) -> p c t", c=2)))

            MG = tv(B_SQ, TT)
            M1 = B_OSB.t[:, 0:TT]
            M2 = B_OSB.t[:, 512:512 + TT]
            for half in range(2):
                wa = ws.get("w_br_a", half * 512)
                wb = ws.get("w_br_b", half * 512)
                for j in range(4):
                    ob = half * 4 + j
                    pa = PB[2 + (ob % 2) * 2]
                    pb = PB[3 + (ob % 2) * 2]
                    for c in range(KC):
                        k.mm(pa.v(pa.t[:, 0:TT]), wa[:, c, j * 128:(j + 1) * 128], B_OG.v(OG[:, c, :]), start=(c == 0), stop=(c == KC - 1))
                    for c in range(KC):
                        k.mm(pb.v(pb.t[:, 0:TT]), wb[:, c, j * 128:(j + 1) * 128], B_EQ.v(CVG[:, c, :]), start=(c == 0), stop=(c == KC - 1))
                    k.stt(B_OSB.v(M1, "m1"), B_F.v(TM["ma"][:, ob, :], "ma"), 1.0, pa.v(pa.t[:, 0:TT]), ALU.add, ALU.mult)
                    k.stt(B_OSB.v(M2, "m2"), B_VS.v(TM["mb"][:, ob, :], "mb"), 1.0, pb.v(pb.t[:, 0:TT]), ALU.add, ALU.mult)
                    k.tt(DVE, B_SQ.v(MG[:, ob, :]), B_OSB.v(M1, "m1"), B_OSB.v(M2, "m2"), ALU.add)
            mark(f'L{l}T{ti} wo')
            merge_all()
            for half in range(2):
                wg = ws.get("w_o", half * 512)
                for j in range(4):
                    ob = half * 4 + j
                    ps = PB[pj % 2]
                    pj += 1
                    for c in range(KC):
                        k.mm(ps.v(ps.t[:, 0:TT]), wg[:, c, j * 128:(j + 1) * 128], B_SQ.v(MG[:, c, :]), start=(c == 0), stop=(c == KC - 1))
                    hv = Hb.k(tkey, (slice(None), ob, slice(t0, t0 + TT)))
                    k.stt(hv, ps.v(ps.t[:, 0:TT]), 0.5, hv, ALU.mult, ALU.add)
                    HSQ = tv(B_OG, TT)
                    pr = PB[2]
                    k.act(B_OG.v(HSQ[:, ob, :], ob), hv, AF.Square)
                    if ob > 0:
                        k.mm(pr.v(pr.t[:, 0:TT]), ones_b[:], B_OG.v(HSQ[:, ob - 1, :], ob - 1), start=(ob == 1), stop=False)
            k.mm(pr.v(pr.t[:, 0:TT]), ones_b[:], B_OG.v(HSQ[:, KC - 1, :], KC - 1), start=False, stop=True)
            k.act(B_RS.v(B_RS.t[:, 0:TT]), pr.v(pr.t[:, 0:TT]), AF.Ln, bias=EPS, scale=1.0 / D)
            k.act(B_RS.v(B_RS.t[:, 0:TT]), B_RS.v(B_RS.t[:, 0:TT]), AF.Exp, scale=-0.5)
            mark(f'L{l}T{ti} ple')
            merge_all()
            for c in range(KC):
                k.stt(B_XN.v(XN[:, c, :]), Hb.k(tkey, (slice(None), c, slice(t0, t0 + TT))), pv(l, "norm_ple", c),
                      B_RS.v(B_RS.t[:, 0:TT]), ALU.mult, ALU.mult)
            TG = tv(B_VS, TT)
            for half in range(2):
                wg = ws.get("w_pg", half * 512)
                for j in range(4):
                    ob = half * 4 + j
                    ps = PB[pj % 2]
                    pj += 1
                    for c in range(KC):
                        k.mm(ps.v(ps.t[:, 0:TT]), wg[:, c, j * 128:(j + 1) * 128], B_XN.v(XN[:, c, :]), start=(c == 0), stop=(c == KC - 1))
                    k.act(B_VS.v(TG[:, ob, :], "tg"), ps.v(ps.t[:, 0:TT]), AF.Tanh, scale=0.5)
            for half in range(2):
                wg = ws.get("w_ple", half * 512)
                for j in range(4):
                    ob = half * 4 + j
                    ps = PB[pj % 2]
                    pj += 1
                    for c in range(2):
                        k.mm(ps.v(ps.t[:, 0:TT]), wg[:, c, j * 128:(j + 1) * 128], B_OSQ.v(PT[:, c, 0:TT]), start=(c == 0), stop=(c == 1))
                    k.stt(B_OSB.v(M1, "m1"), B_VS.v(TG[:, ob, :], "tg"), 1.0, ps.v(ps.t[:, 0:TT]), ALU.add, ALU.mult)
                    hv = Hb.k(tkey, (slice(None), ob, slice(t0, t0 + TT)))
                    k.stt(hv, B_OSB.v(M1, "m1"), 0.5, hv, ALU.mult, ALU.add)
            merge_all()
            nidx = l * len(tiles) + ti + 1
            if nidx < nlayers * len(tiles):
                nti = nidx % len(tiles)
                nt0, nTT, _ = tiles[nti]
                rms_rstd(nt0, nTT, nti, B_KK, KKF[:, 0:nTT])

    mark('final')
    for ti, (t0, TT, typ) in enumerate(tiles):
        NB = TT // 128
        rms_rstd(t0, TT, ti, B_RS)
        YT = B_VS.t[:, 0:KC * TT].bitcast(F32) if False else None
        for blk in range(NB):
            ysb = B_VS.t[:, 0:2048].bitcast(F32).rearrange("p (c t) -> p c t", c=KC)
            for c in range(KC):
                k.stt(B_VS.v(ysb[:, c, :], "ysb"), Hb.k(ti, (slice(None), c, slice(t0 + blk * 128, t0 + (blk + 1) * 128))),
                      nfv[:, c:c + 1], B_RS.v(B_RS.t[:, blk * 128:(blk + 1) * 128]), ALU.mult, ALU.mult)
            YO = B_VS.t[:, 2048:4096].bitcast(F32)
            for half in range(2):
                ps = PB[half]
                for j in range(4):
                    c = half * 4 + j
                    k.tr(ps.v(ps.t[:, j * 128:(j + 1) * 128]), B_VS.v(ysb[:, c, :], "ysb"), ident_f[:])
                k.copy(ACT if half else DVE, B_VS.v(YO[:, half * 512:(half + 1) * 512], "yo"), ps[:])
            dst = yp[t0 + blk * 128:t0 + (blk + 1) * 128, :] if typ == "p" else ys
            k.dma(QS, dst, B_VS.v(YO, "yo"))

    assert ws.used == len(ws.sched), (ws.used, len(ws.sched))
    k.finish()
    es.close()
    return nc, k


_CACHE = {}


def _consts():
    lmn_p, u_p, mask_p = _chunk_consts(64)
    lmn_s, u_s, mask_s = _chunk_consts(8)
    bd = (np.arange(128)[:, None] // 8 == np.arange(16)[None, :]).astype(np.float32)
    sel = np.zeros((4, 4, 128), np.float32)
    for i in range(4):
        sel[i, i, :] = 1.0
    return dict(sel=sel.reshape(4, 512), ident=np.eye(128, dtype=np.float32), lmn_p=lmn_p, lmn_s=lmn_s, u_p=u_p, u_s=u_s,
                mask_p=mask_p, mask_s=mask_s, bd=bd)


def make_in_maps(inp):
    f = lambda a: np.ascontiguousarray(np.asarray(a, dtype=np.float32))
    b_in = f(inp["b_in"])
    bfm = b_in.reshape(DEPTH, 72, 128).transpose(2, 0, 1).reshape(128, DEPTH * 72)
    vecs = np.stack([f(inp[n]) for n in ("norm_mix", "gnorm_a", "conv_b", "ln_g", "ln_b", "norm_ple")], axis=1)
    pvec = vecs.reshape(DEPTH, 6, 8, 128).transpose(3, 0, 1, 2).reshape(128, DEPTH * 6 * 8)
    nf = f(inp["norm_final"]).reshape(8, 128).T
    cw = f(inp["conv_w"]).reshape(DEPTH, CWID, 8, 128).transpose(3, 0, 2, 1).reshape(128, DEPTH * 8 * CWID)
    shared = dict(w_in=f(inp["w_in"]), w_br_a=f(inp["w_br_a"]), w_br_b=f(inp["w_br_b"]), w_o=f(inp["w_o"]),
                  w_pg=f(inp["w_ple_gate"]), w_ple=f(inp["w_ple"]), b_in=b_in, lb_param=f(inp["lb_param"]),
                  bfm=np.ascontiguousarray(bfm), pvec=np.ascontiguousarray(pvec), nf=np.ascontiguousarray(nf),
                  cw=np.ascontiguousarray(cw))
    shared.update(_consts())
    x_prompt, x_sample = f(inp["x_prompt"]), f(inp["x_sample"])
    state_hgrn, state_conv = f(inp["state_hgrn"]), f(inp["state_conv"])
    p_prompt, p_sample = f(inp["p_prompt"]), f(inp["p_sample"])
    maps = []
    for c in range(NCORE):
        sl = slice(c * 16, (c + 1) * 16)
        m = dict(shared)
        m["xp"] = np.ascontiguousarray(x_prompt[c])
        m["xs"] = np.ascontiguousarray(x_sample[sl].reshape(TS, D))
        m["sh"] = np.ascontiguousarray(state_hgrn[:, sl])
        m["sc"] = np.ascontiguousarray(state_conv[:, sl])
        m["pp"] = np.ascontiguousarray(p_prompt[:, c])
        m["pps"] = np.ascontiguousarray(p_sample[:, sl].reshape(DEPTH, TS, PLE))
        maps.append(m)
    return maps


def assemble(results):
    y_prompt = np.stack([r["yp"] for r in results], axis=0)
    y_sample = np.concatenate([r["ys"].reshape(16, 8, D) for r in results], axis=0)
    hgrn_p = np.stack([r["hp"] for r in results], axis=1)
    conv_p = np.stack([r["cp"] for r in results], axis=1)
    hgrn_s = np.concatenate([r["hs"] for r in results], axis=1)
    conv_s = np.concatenate([r["cs"] for r in results], axis=1)
    return tuple(np.ascontiguousarray(a, dtype=np.float32) for a in (y_prompt, y_sample, hgrn_p, conv_p, hgrn_s, conv_s))


def kernel(**inputs):
    if "nc" not in _CACHE:
        _CACHE["nc"] = build_program()[0]
    nc = _CACHE["nc"]
    maps = make_in_maps(inputs)
    res = run_bass_kernel_spmd(nc, maps, core_ids=list(range(NCORE)))
    return assemble(res.results)
```

```python
import contextlib
import numpy as np
import concourse.bass as bass
import concourse.mybir as mybir
from concourse.bass_utils import run_bass_kernel_spmd

F32 = mybir.dt.float32
F32R = mybir.dt.float32r
BF16 = mybir.dt.bfloat16
U32 = mybir.dt.uint32
AF = mybir.ActivationFunctionType
ALU = mybir.AluOpType

D = 1024
KC = 8
NIN = 9216
NH = 8
DEPTH = 4
CWID = 31
HALO = 30
PLE = 256
NCORE = 8
TP = 2048
TS = 128
NT = TP + TS
EPS = 1e-6
OFF = dict(q=0, f=1024, iv=2048, za=3072, ga=4096, gb=5120, zb=6144, ma=7168, mb=8192)
PV = dict(norm_mix=0, gnorm_a=1, conv_b=2, ln_g=3, ln_b=4, norm_ple=5)


class Rec:
    __slots__ = ("w", "r")

    def __init__(self):
        self.w = {}
        self.r = {}


def _mx(d, k, v):
    if d.get(k, 0) < v:
        d[k] = v


class V:
    __slots__ = ("buf", "key", "ap")

    def __init__(self, buf, key, ap):
        self.buf = buf
        self.key = key
        self.ap = ap


class Buf:
    def __init__(self, name, t, psum=False):
        self.name = name
        self.t = t
        self.recs = {}
        self.all = Rec()
        self.psum = psum

    def merge(self):
        for r in self.recs.values():
            for s, v in r.w.items():
                _mx(self.all.w, s, v)
            for s, v in r.r.items():
                _mx(self.all.r, s, v)
        self.recs = {}

    def v(self, ap, key=None):
        return V(self, None if self.psum else key, ap)

    def __getitem__(self, idx):
        return V(self, None, self.t[idx])

    def k(self, key, idx):
        return V(self, None if self.psum else key, self.t[idx])


class Eng:
    def __init__(self, name, e, sem, semkey):
        self.name = name
        self.e = e
        self.sem = sem
        self.semkey = semkey
        self.n = 0
        self.waited = {}


class Queue:
    def __init__(self, eng, sems, keys):
        self.eng = eng
        self.sems = sems
        self.keys = keys
        self.n = 0


class KB:
    def __init__(self, nc, es):
        self.nc = nc
        self.es = es
        self.semh = {}
        self.nsem = 0
        self.PE = self._eng("pe", nc.tensor)
        self.ACT = self._eng("act", nc.scalar)
        self.DVE = self._eng("dve", nc.vector)
        self.POOL = self._eng("pool", nc.gpsimd)
        self.SP = Eng("sp", nc.sync, None, None)
        self.qsp = self._queue(self.SP, 24)
        self.qpool = self._queue(self.POOL, 16)
        self.nops = 0
        self.nwaits = 0

    def _newsem(self, name):
        s = self.es.enter_context(self.nc.semaphore(name))
        k = self.nsem
        self.nsem += 1
        self.semh[k] = s
        return s, k

    def _eng(self, name, e):
        s, k = self._newsem("sem_" + name)
        return Eng(name, e, s, k)

    def _queue(self, eng, n):
        sems, keys = [], []
        for i in range(n):
            s, k = self._newsem(f"dq_{eng.name}_{i}")
            sems.append(s)
            keys.append(k)
        return Queue(eng, sems, keys)

    def sb(self, name, shape, dt):
        return Buf(name, self.es.enter_context(self.nc.sbuf_tensor("s_" + name, list(shape), dt)))

    def ps(self, name, shape, dt=F32):
        return Buf(name, self.es.enter_context(self.nc.psum_tensor("p_" + name, list(shape), dt)), psum=True)

    def _need(self, reads, writes, ownkey):
        need = {}
        for v in reads:
            b = v.buf
            for s, val in b.all.w.items():
                _mx(need, s, val)
            if b.psum:
                for s, val in b.all.r.items():
                    if s != ownkey:
                        _mx(need, s, val)
            if v.key is None:
                for r in b.recs.values():
                    for s, val in r.w.items():
                        _mx(need, s, val)
            else:
                r = b.recs.get(v.key)
                if r is not None:
                    for s, val in r.w.items():
                        _mx(need, s, val)
        for v in writes:
            b = v.buf
            lst = [b.all]
            if v.key is None:
                lst += list(b.recs.values())
            else:
                r = b.recs.get(v.key)
                if r is not None:
                    lst.append(r)
            for r in lst:
                for s, val in r.w.items():
                    if s != ownkey:
                        _mx(need, s, val)
                for s, val in r.r.items():
                    if s != ownkey:
                        _mx(need, s, val)
        return need

    def _commit(self, reads, writes, tk, tv):
        for v in reads:
            b = v.buf
            if v.key is None:
                _mx(b.all.r, tk, tv)
            else:
                r = b.recs.get(v.key)
                if r is None:
                    r = b.recs[v.key] = Rec()
                _mx(r.r, tk, tv)
        for v in writes:
            b = v.buf
            if v.key is None:
                b.recs = {}
                b.all = Rec()
                b.all.w[tk] = tv
            else:
                r = b.recs.get(v.key)
                if r is None:
                    r = b.recs[v.key] = Rec()
                r.w = {tk: tv}
                r.r = {}

    def _wait(self, eng, need):
        for s, val in need.items():
            if eng.waited.get(s, 0) < val:
                eng.e.wait_ge(self.semh[s], val)
                eng.waited[s] = val
                self.nwaits += 1

    def op(self, eng, fn, reads, writes):
        reads = [v for v in reads if isinstance(v, V)]
        writes = [v for v in writes if isinstance(v, V)]
        need = self._need(reads, writes, eng.semkey if eng is self.PE else None)
        self._wait(eng, need)
        inst = fn()
        eng.n += 1
        inst.then_inc(eng.sem, 1)
        self._commit(reads, writes, eng.semkey, eng.n)
        self.nops += 1

    def dma(self, q, out, in_):
        reads = [in_] if isinstance(in_, V) else []
        writes = [out] if isinstance(out, V) else []
        eng = q.eng
        R = len(q.sems)
        slot = q.n % R
        prev = 16 * (q.n // R)
        need = self._need(reads, writes, None)
        if prev > 0:
            _mx(need, q.keys[slot], prev)
        self._wait(eng, need)
        oap = out.ap if isinstance(out, V) else out
        iap = in_.ap if isinstance(in_, V) else in_
        eng.e.dma_start(out=oap, in_=iap).then_inc(q.sems[slot], 16)
        q.n += 1
        self._commit(reads, writes, q.keys[slot], prev + 16)
        self.nops += 1

    def finish(self):
        need = {}
        for q in (self.qsp, self.qpool):
            R = len(q.sems)
            for i in range(min(R, q.n)):
                cnt = (q.n - 1 - i) // R + 1
                need[q.keys[i]] = 16 * cnt
        for e in (self.PE, self.ACT, self.DVE, self.POOL):
            if e.n:
                need[e.semkey] = e.n
        self._wait(self.SP, need)

    @staticmethod
    def _a(x):
        return x.ap if isinstance(x, V) else x

    def mm(self, out, lhsT, rhs, start=True, stop=True):
        nc = self.nc
        self.op(self.PE, lambda: nc.tensor.matmul(out.ap, lhsT.ap, rhs.ap, start=start, stop=stop),
                [lhsT, rhs], [out])

    def tr(self, out, in_, ident):
        nc = self.nc
        self.op(self.PE, lambda: nc.tensor.transpose(out.ap, in_.ap, ident.ap), [in_, ident], [out])

    def act(self, out, in_, func, bias=None, scale=1.0):
        nc = self.nc
        a = self._a
        kw = {}
        if bias is not None:
            kw["bias"] = a(bias)
        self.op(self.ACT, lambda: nc.scalar.activation(out=out.ap, in_=in_.ap, func=func, scale=a(scale), **kw),
                [in_, bias, scale], [out])

    def tt(self, eng, out, in0, in1, op):
        self.op(eng, lambda: eng.e.tensor_tensor(out=out.ap, in0=in0.ap, in1=in1.ap, op=op), [in0, in1], [out])

    def stt(self, out, in0, scalar, in1, op0, op1):
        nc = self.nc
        a = self._a
        self.op(self.DVE, lambda: nc.vector.scalar_tensor_tensor(out=out.ap, in0=in0.ap, scalar=a(scalar),
                                                                 in1=in1.ap, op0=op0, op1=op1),
                [in0, scalar, in1], [out])

    def ts(self, eng, out, in0, s1, s2, op0, op1=None):
        a = self._a
        if op1 is None:
            f = lambda: eng.e.tensor_scalar(out=out.ap, in0=in0.ap, scalar1=a(s1), scalar2=None, op0=op0)
        else:
            f = lambda: eng.e.tensor_scalar(out=out.ap, in0=in0.ap, scalar1=a(s1), scalar2=a(s2), op0=op0, op1=op1)
        self.op(eng, f, [in0, s1, s2], [out])

    def copy(self, eng, out, in_):
        if eng is self.ACT:
            self.act(out, in_, AF.Identity)
        else:
            self.op(eng, lambda: eng.e.tensor_copy(out=out.ap, in_=in_.ap), [in_], [out])

    def cpred(self, out, mask, data):
        nc = self.nc
        self.op(self.DVE, lambda: nc.vector.copy_predicated(out=out.ap, mask=mask.ap, data=data.ap),
                [mask, data, out], [out])

    def memset(self, eng, out, val):
        self.op(eng, lambda: eng.e.memset(out.ap, val), [], [out])

    def recip(self, out, in_):
        nc = self.nc
        self.op(self.DVE, lambda: nc.vector.reciprocal(out=out.ap, in_=in_.ap), [in_], [out])


def _chunk_consts(cl):
    s = np.arange(128)[:, None]
    t = np.arange(128)[None, :]
    same = (s // cl) == (t // cl)
    L = (same & (s <= t)).astype(np.float32)
    m = (t // cl) * cl + cl // 2 - 1
    Lm = (same & (s <= m)).astype(np.float32)
    M2 = L - Lm
    lmn = np.concatenate([L, M2, -M2], axis=1).astype(np.float32)
    U = (same & (s > t)).astype(np.float32)
    mask = (same & (t >= s)).astype(np.uint32)
    return lmn, U, mask


def build_program(nlayers=DEPTH, dbg=None):
    nc = bass.Bass("TRN2", target_bir_lowering=False)
    es = contextlib.ExitStack()

    def din(name, shape, dt=F32):
        return nc.dram_tensor(name, list(shape), dt, kind="ExternalInput").ap()

    def dout(name, shape, dt=F32):
        return nc.dram_tensor(name, list(shape), dt, kind="ExternalOutput").ap()

    xp = din("xp", [TP, D])
    xs = din("xs", [TS, D])
    sh = din("sh", [DEPTH, 16, NH, 128, 128])
    sc = din("sc", [DEPTH, 16, HALO, D])
    pp = din("pp", [DEPTH, TP, PLE])
    pps = din("pps", [DEPTH, TS, PLE])
    w_in = din("w_in", [DEPTH, D, NIN])
    w_br_a = din("w_br_a", [DEPTH, D, D])
    w_br_b = din("w_br_b", [DEPTH, D, D])
    w_o = din("w_o", [DEPTH, D, D])
    w_pg = din("w_pg", [DEPTH, D, D])
    w_ple = din("w_ple", [DEPTH, PLE, D])
    b_in = din("b_in", [DEPTH, NIN])
    lb_param = din("lb_param", [DEPTH, D])
    bfm_d = din("bfm", [128, DEPTH * 72])
    pvec_d = din("pvec", [128, DEPTH * 6 * 8])
    nf_d = din("nf", [128, 8])
    cw_d = din("cw", [128, DEPTH * 8 * CWID])
    ident_d = din("ident", [128, 128])
    lmnp_d = din("lmn_p", [128, 384])
    lmns_d = din("lmn_s", [128, 384])
    up_d = din("u_p", [128, 128])
    us_d = din("u_s", [128, 128])
    maskp_d = din("mask_p", [128, 128], U32)
    masks_d = din("mask_s", [128, 128], U32)
    bd_d = din("bd", [128, 16])
    sel_d = din("sel", [4, 512])

    yp = dout("yp", [TP, D])
    ys = dout("ys", [TS, D])
    hp = dout("hp", [DEPTH, NH, 128, 128])
    cp = dout("cp", [DEPTH, HALO, D])
    hs = dout("hs", [DEPTH, 16, NH, 128, 128])
    cs = dout("cs", [DEPTH, 16, HALO, D])

    k = KB(nc, es)
    PE, ACT, DVE, POOL = k.PE, k.ACT, k.DVE, k.POOL
    QS, QP = k.qsp, k.qpool

    Hb = k.sb("H", [128, KC, NT], F32)
    Sf = k.sb("Sf", [128, NH, 128], F32)
    Sb = k.sb("Sb", [128, NH, 128], BF16)
    ident_f = k.sb("ident_f", [128, 128], F32)
    ident_b = k.sb("ident_b", [128, 128], BF16)
    ones_b = k.sb("ones_b", [128, 128], BF16)
    LMNb = k.sb("LMNb", [128, 384], F32R)
    UMb = k.sb("UMb", [128, 128], F32R)
    MKb = k.sb("MKb", [128, 128], U32)
    B_LF = k.sb("B_LF", [128, 1024], F32R)
    bdm = k.sb("bdm", [128, 16], F32)
    bfm = k.sb("bfm", [128, 72], F32)
    hbfm = k.sb("hbfm", [128, 24], F32)
    pvec = k.sb("pvec", [128, 6 * 8], F32)
    nfv = k.sb("nfv", [128, 8], F32)
    cwv = k.sb("cwv", [128, 8 * CWID], F32)
    OMLB = k.sb("OMLB", [128, D], F32)
    brow = k.sb("brow", [4, 512], BF16)
    selb = k.sb("selb", [4, 4, 128], BF16)

    B_XN = k.sb("B_XN", [128, 4096], BF16)
    B_VS = k.sb("B_VS", [128, 8192], BF16)
    B_SQ = k.sb("B_SQ", [128, 4096], BF16)
    B_OG = k.sb("B_OG", [128, 4096], BF16)
    B_F = k.sb("B_F", [128, 2048], F32)
    B_MISC = k.sb("B_MISC", [128, 1024], F32)
    B_KK = k.sb("B_KK", [128, 1024], BF16)
    B_EQ = k.sb("B_EQ", [128, 3072], F32)
    B_OSB = k.sb("B_OSB", [128, 1024], F32)
    B_OSQ = k.sb("B_OSQ", [128, 1024], BF16)
    B_DC = k.sb("B_DC", [128, NH * 16], F32)
    B_AM = k.sb("B_AM", [128, 8, 128], BF16)
    B_RS = k.sb("B_RS", [128, 512], F32)
    B_U = k.sb("B_U", [128, KC * 608], BF16)
    NRING = 4
    WR = [k.sb(f"WR{i}", [128, KC, 512], BF16) for i in range(NRING)]

    PB = [k.ps(f"PB{i}", [128, 512]) for i in range(8)]

    dbg_out = {}
    if dbg:
        for name, shape in dbg.items():
            dbg_out[name] = dout("dbg_" + name, shape[0], shape[1])

    def dump(name, v):
        if name in dbg_out:
            k.dma(QS, dbg_out[name], v)

    class WS:
        def __init__(self):
            self.sched = []
            self.issued = 0
            self.used = 0

        def issue(self, i):
            src, col0, kc, tag, lyr = self.sched[i]
            b = WR[i % NRING]
            if lyr >= 1:
                sb_, sap_ = scr[(tag[0], lyr)]
                ap = V(sb_, None, sap_[:, col0:col0 + 512].rearrange("(kc p) n -> p kc n", p=128))
            else:
                ap = src[:, col0:col0 + 512].rearrange("(kc p) n -> p kc n", p=128)
            k.dma(QP, b.v(b.t[:, 0:kc, :]), ap)

        def get(self, src_name, col0):
            assert self.used < len(self.sched), "weight schedule exhausted"
            s = self.sched[self.used]
            assert s[3] == (src_name, col0), (s[3], src_name, col0)
            while self.issued < min(len(self.sched), self.used + NRING - 1):
                self.issue(self.issued)
                self.issued += 1
            b = WR[self.used % NRING]
            self.used += 1
            return b

    ws = WS()
    k.marks = []
    WSRC = {"w_in": (w_in, D, NIN), "w_br_a": (w_br_a, D, D), "w_br_b": (w_br_b, D, D), "w_o": (w_o, D, D),
            "w_pg": (w_pg, D, D), "w_ple": (w_ple, PLE, D)}
    scr = {}
    for name_, (src_, rows_, cols_) in WSRC.items():
        for l_ in range(1, nlayers):
            t_ = nc.dram_tensor(f"scr_{name_}_{l_}", [rows_, cols_], BF16, kind="Internal")
            scr[(name_, l_)] = (Buf(f"scr_{name_}_{l_}", t_), t_.ap())

    def precast(ln, q):
        for name_, (src_, rows_, cols_) in WSRC.items():
            b_, ap_ = scr[(name_, ln)]
            r_ = rows_ // 4
            nsub = 4 if name_ == "w_in" else 1
            rs_ = r_ // nsub
            for u_ in range(nsub):
                r0_ = q * r_ + u_ * rs_
                k.dma(QP, V(b_, (q, u_), ap_[r0_:r0_ + rs_, :]), src_[ln, r0_:r0_ + rs_, :])


    def mark(name):
        k.marks.append((name, k.PE.n, k.ACT.n, k.DVE.n, k.POOL.n))

    tiles = [(0, 512, "p"), (512, 512, "p"), (1024, 512, "p"), (1536, 512, "p"), (TP, TS, "s")]
    for l in range(nlayers):
        for _ in tiles:
            def add(src, name, col0, kc=KC):
                ws.sched.append((src[l], col0, kc, (name, col0), l))
            for nm in ("iv", "q", "za", "f", "gb", "zb", "ga"):
                add(w_in, "w_in", OFF[nm])
                add(w_in, "w_in", OFF[nm] + 512)
            for half in range(2):
                add(w_in, "w_in", OFF["ma"] + half * 512)
                add(w_in, "w_in", OFF["mb"] + half * 512)
            for half in range(2):
                add(w_br_a, "w_br_a", half * 512)
                add(w_br_b, "w_br_b", half * 512)
            add(w_o, "w_o", 0)
            add(w_o, "w_o", 512)
            add(w_pg, "w_pg", 0)
            add(w_pg, "w_pg", 512)
            add(w_ple, "w_ple", 0, 2)
            add(w_ple, "w_ple", 512, 2)

    for dst, src in ((ident_f, ident_d), (bdm, bd_d), (nfv, nf_d)):
        k.dma(QS, dst[:], src)

    def load_consts(typ, part="both"):
        lm_d, u_d, m_d = (lmnp_d, up_d, maskp_d) if typ == "p" else (lmns_d, us_d, masks_d)
        if part in ("both", "dma"):
            k.dma(QS, B_MISC.v(B_MISC.t[:, 0:384]), lm_d)
            k.dma(QS, B_MISC.v(B_MISC.t[:, 384:512]), u_d)
            k.dma(QS, MKb[:], m_d)
        if part in ("both", "copy"):
            k.copy(DVE, LMNb[:], B_MISC.v(B_MISC.t[:, 0:384]))
            k.copy(DVE, UMb[:], B_MISC.v(B_MISC.t[:, 384:512]))

    k.copy(DVE, ident_b[:], ident_f[:])
    k.dma(QP, selb.v(selb.t[:, :, :].rearrange('p a b -> p (a b)')), sel_d)
    k.memset(DVE, ones_b[:], 1.0)
    k.memset(DVE, B_AM[:], 0.0)

    def pv(l, name, c):
        i = PV[name] * 8 + c
        return pvec[:, i:i + 1]

    def bcol(l, col):
        return bfm[:, col:col + 1]

    def hbcol(l, col):
        nm_ = {OFF['gb'] // 128: 0, OFF['ma'] // 128: 1, OFF['mb'] // 128: 2}[(col // 8) * 8]
        i = nm_ * 8 + col % 8
        return hbfm[:, i:i + 1]

    def tv(buf, TT, off=0):
        return buf.t[:, off:off + KC * TT].rearrange("p (c t) -> p c t", c=KC)

    def rms_rstd(t0, TT, tkey, rsbuf, rsap=None):
        hsq = tv(B_SQ, TT)
        k.act(B_SQ.v(hsq), Hb.k(tkey, (slice(None), slice(None), slice(t0, t0 + TT))), AF.Square)
        ps = PB[0]
        for c in range(KC):
            k.mm(ps.v(ps.t[:, 0:TT]), ones_b[:], B_SQ.v(hsq[:, c, :]), start=(c == 0), stop=(c == KC - 1))
        if rsap is None:
            rsap = rsbuf.t[:, 0:TT]
        k.act(rsbuf.v(rsap), ps.v(ps.t[:, 0:TT]), AF.Ln, bias=EPS, scale=1.0 / D)
        k.act(rsbuf.v(rsap), rsbuf.v(rsap), AF.Exp, scale=-0.5)

    TILEBUFS = (B_XN, B_VS, B_SQ, B_OG, B_F, B_MISC, B_EQ, B_OSB, B_OSQ, B_U, B_KK)

    def merge_all():
        for b in TILEBUFS:
            b.merge()

    omlb_t = nc.dram_tensor("scr_omlb", [DEPTH, 128, D], F32, kind="Internal")
    omlb_buf, omlb_ap = Buf("scr_omlb", omlb_t), omlb_t.ap()
    LBT = B_VS.t[:, 0:8192].bitcast(F32)
    k.dma(QS, B_VS.v(LBT), lb_param.rearrange("l d -> (l d)").partition_broadcast(128))
    k.act(B_VS.v(LBT), B_VS.v(LBT), AF.Exp)
    ssum = B_OSB.t[:, 0:1024]
    num = B_MISC.t[:, 0:1024]
    k.tt(DVE, B_OSB.v(ssum), B_VS.v(LBT[:, 0:1024]), B_VS.v(LBT[:, 1024:2048]), ALU.add)
    k.tt(DVE, B_OSB.v(ssum), B_OSB.v(ssum), B_VS.v(LBT[:, 2048:3072]), ALU.add)
    k.tt(DVE, B_OSB.v(ssum), B_OSB.v(ssum), B_VS.v(LBT[:, 3072:4096]), ALU.add)
    k.recip(B_OSB.v(ssum), B_OSB.v(ssum))
    for l_ in range(nlayers):
        k.copy(DVE, B_MISC.v(num), B_VS.v(LBT[:, 0:1024]))
        for j in range(l_ + 1, DEPTH):
            k.tt(DVE, B_MISC.v(num), B_MISC.v(num), B_VS.v(LBT[:, j * 1024:(j + 1) * 1024]), ALU.add)
        k.tt(DVE, OMLB[:], B_MISC.v(num), B_OSB.v(ssum), ALU.mult)
        k.dma(QS, V(omlb_buf, l_, omlb_ap[l_]), OMLB[:])

    XST = B_MISC
    for blk in range(NT // 128):
        src = xp[blk * 128:(blk + 1) * 128, :] if blk < 16 else xs
        k.dma(QS, XST[:], src)
        for half in range(2):
            ps = PB[half]
            for j in range(4):
                c = half * 4 + j
                k.tr(ps.v(ps.t[:, j * 128:(j + 1) * 128]), XST.v(XST.t[:, c * 128:(c + 1) * 128]), ident_f[:])
            tkey = min(blk // 4, 4)
            k.copy(ACT if half else DVE,
                   Hb.k(tkey, (slice(None), slice(half * 4, half * 4 + 4), slice(blk * 128, (blk + 1) * 128))),
                   ps.v(ps.t[:, :].rearrange("p (c t) -> p c t", c=4)))

    KKF = B_KK.t[:, 0:1024].bitcast(F32)
    rms_rstd(0, 512, 0, B_KK, KKF[:, 0:512])
    for l in range(nlayers):
        k.dma(QS, OMLB[:], V(omlb_buf, l, omlb_ap[l]))
        k.dma(QP, brow[:], b_in[l, 1024:3072].rearrange('(r n) -> r n', r=4))
        k.dma(QS, cwv[:], cw_d[:, l * 8 * CWID:(l + 1) * 8 * CWID])
        k.dma(QS, bfm[:], bfm_d[:, l * 72:(l + 1) * 72])
        k.dma(QS, pvec[:], pvec_d[:, l * 48:(l + 1) * 48])
        load_consts("p", "dma")
        k.memset(POOL, Sf[:], 0.0)
        k.memset(POOL, Sb[:], 0.0)
        k.memset(POOL, B_U[:], 0.0)

        def deferred_layer_setup():
            for i_, nm_ in enumerate(('gb', 'ma', 'mb')):
                c0_ = OFF[nm_] // 128
                k.ts(DVE, hbfm.v(hbfm.t[:, i_ * 8:i_ * 8 + 8]), bfm.v(bfm.t[:, c0_:c0_ + 8]), 0.5, None, ALU.mult)
            load_consts("p", "copy")
            k.memset(DVE, B_AM[:], 0.0)

        for ti, (t0, TT, typ) in enumerate(tiles):
            NB = TT // 128
            tkey = ti
            CL = 64 if typ == "p" else 8
            NCH = 128 // CL
            merge_all()
            if typ == "s":
                load_consts("s")
                k.memset(DVE, B_AM[:], 0.0)
            XN = tv(B_XN, TT)
            VTM = B_VS.t[:, 0:NB * 1024].rearrange("p (b n) -> p b n", b=NB)
            SZA = tv(B_VS, TT, off=NB * 1024)
            SQ = tv(B_SQ, TT)
            OG = tv(B_OG, TT)
            Hsl = (slice(None), slice(None), slice(t0, t0 + TT))

            mark(f'L{l}T{ti} start')
            for c in range(KC):
                k.stt(B_XN.v(XN[:, c, :]), Hb.k(tkey, (slice(None), c, slice(t0, t0 + TT))), pv(l, "norm_mix", c),
                      B_KK.v(KKF[:, 0:TT]), ALU.mult, ALU.mult)
            if ti == 0:
                deferred_layer_setup()

            mark(f'L{l}T{ti} iv')
            pj = 0
            for half in range(2):
                wg = ws.get("w_in", OFF["iv"] + half * 512)
                for blk in range(NB):
                    ps = PB[pj % 2]
                    pj += 1
                    k.mm(ps[:], selb[0:4, 2 + half, :], brow[0:4, :], start=True, stop=False)
                    for c in range(KC):
                        k.mm(ps[:], B_XN.v(XN[:, c, blk * 128:(blk + 1) * 128]), wg[:, c, :], start=False, stop=(c == KC - 1))
                    k.copy(ACT, B_VS.v(VTM[:, blk, half * 512:(half + 1) * 512], 'vtm'), ps[:])

            for nm, dstb, dst, dkey in (("q", B_SQ, SQ, None), ("za", B_VS, SZA, "sza")):
                for half in range(2):
                    wg = ws.get("w_in", OFF[nm] + half * 512)
                    for j in range(4):
                        hd = half * 4 + j
                        ps = PB[pj % 2]
                        pj += 1
                        for c in range(KC):
                            k.mm(ps.v(ps.t[:, 0:TT]), wg[:, c, j * 128:(j + 1) * 128], B_XN.v(XN[:, c, :]),
                                 start=(c == 0), stop=(c == KC - 1))
                        k.act(dstb.v(dst[:, hd, :], dkey), ps.v(ps.t[:, 0:TT]), AF.Silu, bias=bcol(l, OFF[nm] // 128 + hd))

            mark(f'L{l}T{ti} hgrn')
            if typ == 'p' and l + 1 < nlayers:
                precast(l + 1, ti)
            wf = [ws.get("w_in", OFF["f"]), ws.get("w_in", OFF["f"] + 512)]
            LFR = B_LF.t[:, 0:1024]
            LF = LFR
            E4 = B_MISC.t[:, 0:1024]
            KK = B_KK.t[:, 0:1024]
            QBQ = B_EQ.t[:, 1536:3072].bitcast(BF16).rearrange("p (h j t) -> p h j t", h=NH, j=3)
            OSB = B_OSB.t[:, 0:1024].rearrange("p (h t) -> p h t", h=NH)
            OSQ = B_OSQ.t[:, 0:1024].rearrange("p (h t) -> p h t", h=NH)
            DC = B_DC.t[:, :].rearrange("p (h c) -> p h c", h=NH)
            LMN = LMNb
            UM = UMb
            MK = MKb
            THKs = [B_F.t[:, 0:1024], B_F.t[:, 1024:2048]]

            def f_pre_a(blk):
                nonlocal pj
                bs_ = slice(blk * 128, (blk + 1) * 128)
                thk_ = THKs[blk % 2]
                tk_ = "thk%d" % (blk % 2)
                for half in range(2):
                    ps = PB[pj % 2]
                    pj += 1
                    k.mm(ps[:], selb[0:4, half, :], brow[0:4, :], start=True, stop=False)
                    for c in range(KC):
                        k.mm(ps[:], B_XN.v(XN[:, c, bs_]), wf[half][:, c, :], start=False, stop=(c == KC - 1))
                    th = B_F.v(thk_[:, half * 512:(half + 1) * 512], tk_)
                    k.act(th, ps[:], AF.Exp)
                    k.act(th, th, AF.Ln, bias=1.0)
                    k.act(th, th, AF.Exp, scale=-1.0)

            def f_pre_b(blk):
                thk_ = THKs[blk % 2]
                tk_ = "thk%d" % (blk % 2)
                k.tt(DVE, B_F.v(thk_, tk_), B_F.v(thk_, tk_), OMLB[:], ALU.mult)

            f_pre_a(0)
            f_pre_b(0)
            for blk in range(NB):
                bs = slice(blk * 128, (blk + 1) * 128)
                THK = THKs[blk % 2]
                TKEY = "thk%d" % (blk % 2)
                k.act(B_LF.v(LFR, "lf"), B_F.v(THK, TKEY), AF.Ln, bias=1.0, scale=-1.0)
                for half in range(2):
                    ps = PB[half]
                    k.mm(ps[:], UM[:], B_LF.v(LFR[:, half * 512:(half + 1) * 512], "lf"))
                    k.act(B_MISC.v(E4[:, half * 512:(half + 1) * 512]), ps[:], AF.Exp)
                k.tt(DVE, B_KK.v(KK), B_F.v(THK, TKEY), B_MISC.v(E4), ALU.mult)
                if l == 0 and typ == 's':
                    dump('ktm', B_F.v(THK, TKEY))
                    dump('lf', B_LF.v(LF, 'lf'))
                    dump('e4', B_MISC.v(E4))
                    dump('vtm', B_VS.v(VTM[:, 0, :]))
                    dump('xn', B_XN.v(XN))
                for g in range(4):
                    Eg = B_EQ.t[:, (g % 2) * 768:(g % 2 + 1) * 768].rearrange("p (h n) -> p h n", h=2)
                    for j in range(2):
                        hd = 2 * g + j
                        ekey = "E%d_%d" % (g % 2, j)
                        ps = PB[2 + hd % 4]
                        k.mm(ps.v(ps.t[:, 0:384]), B_LF.v(LFR[:, hd * 128:(hd + 1) * 128], "lf"), LMN[:])
                        k.tr(ps.v(ps.t[:, 384:512]), B_F.v(THK[:, hd * 128:(hd + 1) * 128], TKEY), ident_f[:])
                        k.act(B_EQ.v(Eg[:, j, :], ekey), ps.v(ps.t[:, 0:384]), AF.Exp)
                        k.tt(DVE, B_EQ.v(QBQ[:, hd, 0:2, :], "qbq"),
                             B_SQ.v(SQ[:, hd, bs].unsqueeze(1).broadcast_to([128, 2, 128])),
                             B_EQ.v(Eg[:, j, 0:256].rearrange("p (j t) -> p j t", j=2), ekey), ALU.mult)
                        k.tt(DVE, B_EQ.v(QBQ[:, hd, 2, :], "qbq"), ps.v(ps.t[:, 384:512]),
                             B_EQ.v(Eg[:, j, 256:384], ekey), ALU.mult)
                        k.copy(POOL, B_DC.v(DC[:, hd, 0:NCH]),
                               B_EQ.v(Eg[:, j, 0:128].rearrange("p (c t) -> p c t", t=CL)[:, :, CL - 1], ekey))

                if blk + 1 < NB:
                    f_pre_a(blk + 1)
                if l == 0 and typ == 's':
                    dump('dc', B_DC[:])
                    dump('qbq', B_EQ.v(QBQ))
                def oreg(hd, lo, hi):
                    po = PB[6 + hd // 4]
                    return po.v(po.t[:, (hd % 4) * 128 + lo:(hd % 4) * 128 + hi])

                def sfv(hd):
                    return Sf.k(hd, (slice(None), hd, slice(None)))

                def sbv(hd):
                    return Sb.k(hd, (slice(None), hd, slice(None)))

                for hd in range(NH):
                    pa = PB[4 + hd % 2]
                    k.mm(pa.v(pa.t[:, 0:128]), B_EQ.v(QBQ[:, hd, 2, :], "qbq"), B_EQ.v(QBQ[:, hd, 1, :], "qbq"))
                    k.cpred(B_AM.k(hd, (slice(None), hd, slice(None))), MK[:], pa.v(pa.t[:, 0:128]))
                if typ == "p":
                    for c in range(NCH):
                        cs_ = slice(c * CL, (c + 1) * CL)
                        for hd in range(NH):
                            hs_ = slice(hd * 128, (hd + 1) * 128)
                            ov = oreg(hd, c * CL, (c + 1) * CL)
                            k.mm(ov, sbv(hd), B_EQ.v(QBQ[:, hd, 0, cs_], "qbq"), start=True, stop=False)
                            k.mm(ov, B_VS.v(VTM[cs_, blk, hs_], 'vtm'), B_AM.k(hd, (cs_, hd, cs_)), start=False, stop=True)
                            pa = PB[4 + hd % 2]
                            sv = pa.v(pa.t[:, 0:128])
                            k.mm(sv, B_KK.v(KK[cs_, hs_]), B_VS.v(VTM[cs_, blk, hs_], 'vtm'))
                            k.stt(sfv(hd), sfv(hd), B_DC.v(DC[:, hd, c:c + 1]), sv, ALU.mult, ALU.add)
                            k.copy(POOL, sbv(hd), sfv(hd))
                else:
                    S0fB = [B_VS, B_U]
                    S0f = [B_VS.t[:, 2048:6144].bitcast(F32).rearrange("p (s v) -> p s v", s=16),
                           B_U.t[:, 0:4096].bitcast(F32).rearrange("p (s v) -> p s v", s=16)]
                    S0bB = [B_VS, B_F]
                    S0b = [B_VS.t[:, 6144:8192].rearrange("p (s v) -> p s v", s=16),
                           B_F.t[:, 1024:2048].bitcast(BF16).rearrange("p (s v) -> p s v", s=16)]
                    VBDs = [(B_XN, B_XN.t[:, 1024:3072].rearrange("p (s v) -> p s v", s=16)),
                            (B_EQ, B_EQ.t[:, 0:1024].bitcast(BF16).rearrange("p (s v) -> p s v", s=16))]
                    SOUT = [B_SQ.t[:, 1024:3072].bitcast(F32).rearrange("p (s v) -> p s v", s=8),
                            B_OG.t[:, 1024:3072].bitcast(F32).rearrange("p (s v) -> p s v", s=8)]
                    SOB = [B_SQ, B_OG]

                    def load_state(hd):
                        i = hd % 2
                        k.dma(QP, S0fB[i].v(S0f[i], "s0f"), sh[l, :, hd].rearrange("s k v -> k s v"))

                    def cast_state(hd):
                        i = hd % 2
                        k.copy(ACT, S0bB[i].v(S0b[i], "s0b"), S0fB[i].v(S0f[i], "s0f"))

                    def build_vbd(hd):
                        vb_, vbd_ = VBDs[hd % 2]
                        hsl_ = slice(hd * 128, (hd + 1) * 128)
                        k.tt(POOL, vb_.v(vbd_, "vbd"), B_VS.v(VTM[:, 0, hsl_].unsqueeze(1).broadcast_to([128, 16, 128]), 'vtm'),
                             bdm.v(bdm.t[:, :].unsqueeze(2).broadcast_to([128, 16, 128])), ALU.mult)

                    B_EQ.merge()
                    load_state(0)
                    cast_state(0)
                    build_vbd(0)
                    for hd in range(NH):
                        hs_ = slice(hd * 128, (hd + 1) * 128)
                        i = hd % 2
                        VBDb, VBD = VBDs[hd % 2]
                        if hd + 1 < NH:
                            load_state(hd + 1)
                            build_vbd(hd + 1)
                        k.mm(oreg(hd, 0, 128), B_VS.v(VTM[:, 0, hs_], 'vtm'), B_AM.k(hd, (slice(None), hd, slice(None))),
                             start=True, stop=False)
                        for s_ in range(16):
                            k.mm(oreg(hd, s_ * 8, (s_ + 1) * 8), S0bB[i].v(S0b[i][:, s_, :], "s0b"),
                                 B_EQ.v(QBQ[:, hd, 0, s_ * 8:(s_ + 1) * 8], "qbq"), start=False, stop=(s_ == 15))
                        for hf in range(2):
                            so = SOUT[hf]
                            sob = SOB[hf]
                            for qd in range(2):
                                q4 = hf * 2 + qd
                                ps = PB[q4]
                                k.mm(ps[:], B_KK.v(KK[:, hs_]),
                                     VBDb.v(VBD[:, q4 * 4:(q4 + 1) * 4, :], "vbd"))
                                for s4 in range(4):
                                    s_ = q4 * 4 + s4
                                    k.stt(sob.v(so[:, qd * 4 + s4, :], "sout"), S0fB[i].v(S0f[i][:, s_, :], "s0f"),
                                          B_DC.v(DC[:, hd, s_:s_ + 1]), ps.v(ps.t[:, s4 * 128:(s4 + 1) * 128]),
                                          ALU.mult, ALU.add)
                            k.dma(QS, hs[l, hf * 8:(hf + 1) * 8, hd].rearrange("s k v -> k s v"), sob.v(so, "sout"))
                        if hd + 1 < NH:
                            cast_state(hd + 1)
                if blk + 1 < NB:
                    f_pre_b(blk + 1)
                for hb in range(2):
                    po = PB[6 + hb]
                    k.act(B_OSB.v(B_OSB.t[:, hb * 512:(hb + 1) * 512]), po[:], AF.Identity)
                    k.act(B_OSQ.v(B_OSQ.t[:, hb * 512:(hb + 1) * 512]), po[:], AF.Square)

                RSO = B_MISC.t[:, 0:1024]
                for half in range(2):
                    ps = PB[half]
                    k.mm(ps[:], ones_b[:], B_OSQ.v(B_OSQ.t[:, half * 512:(half + 1) * 512]))
                    rs = B_MISC.v(RSO[:, half * 512:(half + 1) * 512])
                    k.act(rs, ps[:], AF.Ln, bias=EPS, scale=1.0 / 128)
                    k.act(rs, rs, AF.Exp, scale=-0.5)
                k.tt(DVE, B_OSB.v(B_OSB.t[:, 0:1024]), B_OSB.v(B_OSB.t[:, 0:1024]), B_MISC.v(RSO), ALU.mult)
                for hd in range(NH):
                    k.stt(B_OG.v(OG[:, hd, bs]), B_OSB.v(OSB[:, hd, :]), pv(l, "gnorm_a", hd), B_VS.v(SZA[:, hd, bs], 'sza'),
                          ALU.mult, ALU.mult)

            if typ == "p" and ti == 3:
                k.dma(QS, hp[l].rearrange("h k v -> k h v"), Sf[:])

            mark(f'L{l}T{ti} conv')
            merge_all()
            if typ == "p":
                UW = 30 + TT
                Uv = B_U.t[:, 0:KC * UW].rearrange("p (c t) -> p c t", c=KC)
            else:
                Uv = B_U.t[:, 0:KC * 608].rearrange("p (c s t) -> p c s t", c=KC, s=16)
                for q4 in range(4):
                    CST = B_MISC
                    k.dma(QS, CST.v(CST.t[0:120, 0:1024]), sc[l, q4 * 4:(q4 + 1) * 4].rearrange("s t d -> (s t) d"))
                    for half in range(2):
                        ps = PB[half]
                        for j in range(4):
                            cb = half * 4 + j
                            k.tr(ps.v(ps.t[:, j * 128:j * 128 + 120]), CST.v(CST.t[0:120, cb * 128:(cb + 1) * 128]),
                                 ident_f[0:120, 0:120])
                        k.copy(DVE, B_U.v(Uv[:, half * 4:half * 4 + 4, q4 * 4:(q4 + 1) * 4, 0:30]),
                               ps.v(ps.t[:, :].rearrange("p (c x) -> p c x", c=4)[:, :, 0:120]
                                    .rearrange("p c (s t) -> p c s t", s=4)))
                k.dma(QS, cs[l, :, 0:22, :], sc[l, :, 8:30, :])
            CV = tv(B_VS, TT)
            SIG = tv(B_VS, TT, off=4096)
            SZB = tv(B_SQ, TT)
            for nm in ("gb", "zb"):
                for half in range(2):
                    wg = ws.get("w_in", OFF[nm] + half * 512)
                    for j in range(4):
                        cb = half * 4 + j
                        ps = PB[pj % 2]
                        pj += 1
                        for c in range(KC):
                            k.mm(ps.v(ps.t[:, 0:TT]), wg[:, c, j * 128:(j + 1) * 128], B_XN.v(XN[:, c, :]),
                                 start=(c == 0), stop=(c == KC - 1))
                        if nm == "gb":
                            k.act(B_VS.v(SIG[:, cb, :], "sig"), ps.v(ps.t[:, 0:TT]), AF.Tanh, bias=hbcol(l, OFF[nm] // 128 + cb), scale=0.5)
                            k.ts(DVE, B_VS.v(SIG[:, cb, :], "sig"), B_VS.v(SIG[:, cb, :], "sig"), 0.5, 0.5, ALU.mult, ALU.add)
                        else:
                            k.act(B_SQ.v(SZB[:, cb, :]), ps.v(ps.t[:, 0:TT]), AF.Silu, bias=bcol(l, OFF[nm] // 128 + cb))
            Dgs = [(B_F, B_F.t[:, 0:CWID * 64].bitcast(BF16).rearrange("p (j c) -> p j c", j=CWID)),
                   (B_EQ, B_EQ.t[:, 0:CWID * 64].bitcast(BF16).rearrange("p (j c) -> p j c", j=CWID))]

            def build_d(cb):
                db, dg = Dgs[cb % 2]
                cw0 = cb * CWID
                k.tt(DVE, db.v(dg), ident_b.v(ident_b.t[:, :].unsqueeze(1).broadcast_to([128, CWID, 128])),
                     cwv.v(cwv.t[:, cw0:cw0 + CWID].unsqueeze(2).broadcast_to([128, CWID, 128])), ALU.mult)

            build_d(0)
            B_UF = B_MISC
            UF = B_MISC.t[:, 0:256].rearrange('p (c t) -> p c t', c=KC)
            s1 = PB[4]
            s2 = PB[5]
            gaw = {}

            def ga_u(cb):
                nonlocal pj
                half, j = cb // 4, cb % 4
                if j == 0:
                    gaw[half] = ws.get("w_in", OFF["ga"] + half * 512)
                wg = gaw[half]
                ps = PB[pj % 2]
                pj += 1
                for c in range(KC):
                    k.mm(ps.v(ps.t[:, 0:TT]), wg[:, c, j * 128:(j + 1) * 128], B_XN.v(XN[:, c, :]),
                         start=(c == 0), stop=(c == KC - 1))
                gab = bcol(l, OFF["ga"] // 128 + cb)
                if typ == "p":
                    udst = Uv[:, cb, 30:30 + TT]
                    k.stt(B_U.v(udst, cb), ps.v(ps.t[:, 0:TT]), gab, B_VS.v(SIG[:, cb, :], "sig"), ALU.add, ALU.mult)
                    if ti == 3:
                        k.stt(B_UF.v(UF[:, cb, :]), ps.v(ps.t[:, TT - 32:TT]), gab, B_VS.v(SIG[:, cb, TT - 32:TT], "sig"),
                              ALU.add, ALU.mult)
                else:
                    udst = Uv[:, cb, :, 30:38]
                    k.stt(B_U.v(udst, cb), ps.v(ps.t[:, 0:TT].rearrange("p (s t) -> p s t", s=16)), gab,
                          B_VS.v(SIG[:, cb, :].rearrange("p (s t) -> p s t", s=16), "sig"), ALU.add, ALU.mult)
                    k.stt(B_OSB.v(B_OSB.t[:, cb * 128:(cb + 1) * 128]), ps.v(ps.t[:, 0:TT]), gab,
                          B_VS.v(SIG[:, cb, :], "sig"), ALU.add, ALU.mult)

            def conv_cb(cb):
                if cb + 1 < KC:
                    build_d(cb + 1)
                DgB, Dg = Dgs[cb % 2]
                pc = PB[2 + cb % 2]
                for tap in range(CWID):
                    if typ == "p":
                        rhs = Uv[:, cb, tap:tap + TT]
                    else:
                        rhs = Uv[:, cb, :, tap:tap + 8]
                    k.mm(pc.v(pc.t[:, 0:TT]), DgB.v(Dg[:, tap, :]), B_U.v(rhs, cb), start=(tap == 0), stop=(tap == CWID - 1))
                k.act(B_VS.v(CV[:, cb, :], "cv%d" % cb), pc.v(pc.t[:, 0:TT]), AF.Identity, bias=pv(l, "conv_b", cb))
                csq = B_OSQ.t[:, (cb % 2) * 512:(cb % 2) * 512 + TT]
                k.act(B_OSQ.v(csq, "csq%d" % (cb % 2)), pc.v(pc.t[:, 0:TT]), AF.Square, bias=pv(l, "conv_b", cb))
                if cb > 0:
                    conv_stats(cb - 1)

            def conv_stats(cb):
                csq = B_OSQ.t[:, (cb % 2) * 512:(cb % 2) * 512 + TT]
                k.mm(s1.v(s1.t[:, 0:TT]), ones_b[:], B_VS.v(CV[:, cb, :], "cv%d" % cb), start=(cb == 0), stop=(cb == KC - 1))
                k.mm(s2.v(s2.t[:, 0:TT]), ones_b[:], B_OSQ.v(csq, "csq%d" % (cb % 2)), start=(cb == 0), stop=(cb == KC - 1))

            ga_u(0)
            for cb in range(KC):
                if cb + 1 < KC:
                    ga_u(cb + 1)
                conv_cb(cb)
            conv_stats(KC - 1)
            if typ == "p":
                k.copy(ACT, B_U.v(Uv[:, :, 0:30]), B_U.v(Uv[:, :, TT:TT + 30]))
                if ti == 3:
                    ps = PB[0]
                    for cb in range(KC):
                        pst = PB[cb // 4]
                        k.tr(pst.v(pst.t[0:32, (cb % 4) * 128:(cb % 4 + 1) * 128]), B_UF.v(UF[:, cb, :]), ident_f[:])
                    for half in range(2):
                        pst = PB[half]
                        k.copy(DVE, B_MISC.v(B_MISC.t[0:32, half * 512:(half + 1) * 512]), pst.v(pst.t[0:32, :]))
                    k.dma(QS, cp[l], B_MISC.v(B_MISC.t[2:32, 0:1024]))
            else:
                for cb in range(KC):
                    pst = PB[cb // 4]
                    k.tr(pst.v(pst.t[:, (cb % 4) * 128:(cb % 4 + 1) * 128]), B_OSB.v(B_OSB.t[:, cb * 128:(cb + 1) * 128]), ident_f[:])
                for half in range(2):
                    pst = PB[half]
                    k.copy(DVE, B_MISC.v(B_MISC.t[:, half * 512:(half + 1) * 512]), pst[:])
                for sq_ in range(16):
                    k.dma(QS, cs[l, sq_, 22:30, :], B_MISC.v(B_MISC.t[sq_ * 8:(sq_ + 1) * 8, 0:1024]))
            mark(f'L{l}T{ti} ln')
            merge_all()
            psrc = pp[l] if typ == "p" else pps[l]
            pr0 = t0 if typ == "p" else 0
            k.dma(QS, B_MISC.v(B_MISC.t[:, 0:NB * PLE].rearrange("p (b d) -> p b d", b=NB)),
                  psrc[pr0:pr0 + TT, :].rearrange("(b p) d -> p b d", p=128))
            B_RS2 = B_KK
            MU = B_KK.t[:, 0:1024].bitcast(F32)[:, 0:TT]
            RS = B_RS.t[:, 0:TT]
            k.ts(DVE, B_RS2.v(MU), s1.v(s1.t[:, 0:TT]), 1.0 / D, None, ALU.mult)
            k.tt(DVE, B_RS.v(RS), B_RS2.v(MU), B_RS2.v(MU), ALU.mult)
            k.stt(B_RS.v(RS), s2.v(s2.t[:, 0:TT]), 1.0 / D, B_RS.v(RS), ALU.mult, ALU.subtract)
            k.act(B_RS.v(RS), B_RS.v(RS), AF.Ln, bias=EPS)
            k.act(B_RS.v(RS), B_RS.v(RS), AF.Exp, scale=-0.5)
            CVG = B_EQ.t[:, 0:KC * TT // 2].bitcast(BF16).rearrange("p (c t) -> p c t", c=KC)
            XH = B_OSB.t[:, 0:TT]
            TMbuf = {"ma": B_F, "mb": B_VS}
            TM = {"ma": B_F.t[:, 0:KC * TT // 2].bitcast(BF16).rearrange("p (c t) -> p c t", c=KC), "mb": tv(B_VS, TT, off=4096)}
            wgm = {}
            for i_ in range(KC):
                cb = i_
                k.tt(DVE, B_OSB.v(XH), B_VS.v(CV[:, cb, :], "cv"), B_RS2.v(MU), ALU.subtract)
                k.tt(DVE, B_OSB.v(XH), B_OSB.v(XH), B_RS.v(RS), ALU.mult)
                k.act(B_VS.v(CV[:, cb, :], "cv"), B_OSB.v(XH), AF.Silu, bias=pv(l, "ln_b", cb), scale=pv(l, "ln_g", cb))
                k.tt(DVE, B_EQ.v(CVG[:, cb, :]), B_VS.v(CV[:, cb, :], "cv"), B_SQ.v(SZB[:, cb, :]), ALU.mult)
                ob = i_
                half, j = ob // 4, ob % 4
                if j == 0:
                    wgm["ma"] = ws.get("w_in", OFF["ma"] + half * 512)
                    wgm["mb"] = ws.get("w_in", OFF["mb"] + half * 512)
                for nm in ("ma", "mb"):
                    wg = wgm[nm]
                    ps = PB[pj % 2]
                    pj += 1
                    for c in range(KC):
                        k.mm(ps.v(ps.t[:, 0:TT]), wg[:, c, j * 128:(j + 1) * 128], B_XN.v(XN[:, c, :]),
                             start=(c == 0), stop=(c == KC - 1))
                    k.act(TMbuf[nm].v(TM[nm][:, ob, :], nm), ps.v(ps.t[:, 0:TT]), AF.Tanh, bias=hbcol(l, OFF[nm] // 128 + ob), scale=0.5)

            mark(f'L{l}T{ti} merge')
            PT = B_OSQ.t[:, 0:1024].rearrange('p (c t) -> p c t', c=2)
            for blk in range(NB):
                ps = PB[2 + blk % 2]
                for c in range(2):
                    k.tr(ps.v(ps.t[:, c * 128:(c + 1) * 128]), B_MISC.v(B_MISC.t[:, blk * PLE + c * 128:blk * PLE + (c + 1) * 128]), ident_f[:])
                k.copy(DVE, B_OSQ.v(PT[:, :, blk * 128:(blk + 1) * 128]),
                       ps.v(ps.t[:, 0:256].rearrange("p (c t) -> p c t", c=2)))

            MG = tv(B_SQ, TT)
            M1 = B_OSB.t[:, 0:TT]
            M2 = B_OSB.t[:, 512:512 + TT]
            for half in range(2):
                wa = ws.get("w_br_a", half * 512)
                wb = ws.get("w_br_b", half * 512)
                for j in range(4):
                    ob = half * 4 + j
                    pa = PB[2 + (ob % 2) * 2]
                    pb = PB[3 + (ob % 2) * 2]
                    for c in range(KC):
                        k.mm(pa.v(pa.t[:, 0:TT]), wa[:, c, j * 128:(j + 1) * 128], B_OG.v(OG[:, c, :]), start=(c == 0), stop=(c == KC - 1))
                    for c in range(KC):
                        k.mm(pb.v(pb.t[:, 0:TT]), wb[:, c, j * 128:(j + 1) * 128], B_EQ.v(CVG[:, c, :]), start=(c == 0), stop=(c == KC - 1))
                    k.stt(B_OSB.v(M1, "m1"), B_F.v(TM["ma"][:, ob, :], "ma"), 1.0, pa.v(pa.t[:, 0:TT]), ALU.add, ALU.mult)
                    k.stt(B_OSB.v(M2, "m2"), B_VS.v(TM["mb"][:, ob, :], "mb"), 1.0, pb.v(pb.t[:, 0:TT]), ALU.add, ALU.mult)
                    k.tt(DVE, B_SQ.v(MG[:, ob, :]), B_OSB.v(M1, "m1"), B_OSB.v(M2, "m2"), ALU.add)
            mark(f'L{l}T{ti} wo')
            merge_all()
            for half in range(2):
                wg = ws.get("w_o", half * 512)
                for j in range(4):
                    ob = half * 4 + j
                    ps = PB[pj % 2]
                    pj += 1
                    for c in range(KC):
                        k.mm(ps.v(ps.t[:, 0:TT]), wg[:, c, j * 128:(j + 1) * 128], B_SQ.v(MG[:, c, :]), start=(c == 0), stop=(c == KC - 1))
                    hv = Hb.k(tkey, (slice(None), ob, slice(t0, t0 + TT)))
                    k.stt(hv, ps.v(ps.t[:, 0:TT]), 0.5, hv, ALU.mult, ALU.add)
                    HSQ = tv(B_OG, TT)
                    pr = PB[2]
                    k.act(B_OG.v(HSQ[:, ob, :], ob), hv, AF.Square)
                    if ob > 0:
                        k.mm(pr.v(pr.t[:, 0:TT]), ones_b[:], B_OG.v(HSQ[:, ob - 1, :], ob - 1), start=(ob == 1), stop=False)
            k.mm(pr.v(pr.t[:, 0:TT]), ones_b[:], B_OG.v(HSQ[:, KC - 1, :], KC - 1), start=False, stop=True)
            k.act(B_RS.v(B_RS.t[:, 0:TT]), pr.v(pr.t[:, 0:TT]), AF.Ln, bias=EPS, scale=1.0 / D)
            k.act(B_RS.v(B_RS.t[:, 0:TT]), B_RS.v(B_RS.t[:, 0:TT]), AF.Exp, scale=-0.5)
            mark(f'L{l}T{ti} ple')
            merge_all()
            for c in range(KC):
                k.stt(B_XN.v(XN[:, c, :]), Hb.k(tkey, (slice(None), c, slice(t0, t0 + TT))), pv(l, "norm_ple", c),
                      B_RS.v(B_RS.t[:, 0:TT]), ALU.mult, ALU.mult)
            TG = tv(B_VS, TT)
            for half in range(2):
                wg = ws.get("w_pg", half * 512)
                for j in range(4):
                    ob = half * 4 + j
                    ps = PB[pj % 2]
                    pj += 1
                    for c in range(KC):
                        k.mm(ps.v(ps.t[:, 0:TT]), wg[:, c, j * 128:(j + 1) * 128], B_XN.v(XN[:, c, :]), start=(c == 0), stop=(c == KC - 1))
                    k.act(B_VS.v(TG[:, ob, :], "tg"), ps.v(ps.t[:, 0:TT]), AF.Tanh, scale=0.5)
            for half in range(2):
                wg = ws.get("w_ple", half * 512)
                for j in range(4):
                    ob = half * 4 + j
                    ps = PB[pj % 2]
                    pj += 1
                    for c in range(2):
                        k.mm(ps.v(ps.t[:, 0:TT]), wg[:, c, j * 128:(j + 1) * 128], B_OSQ.v(PT[:, c, 0:TT]), start=(c == 0), stop=(c == 1))
                    k.stt(B_OSB.v(M1, "m1"), B_VS.v(TG[:, ob, :], "tg"), 1.0, ps.v(ps.t[:, 0:TT]), ALU.add, ALU.mult)
                    hv = Hb.k(tkey, (slice(None), ob, slice(t0, t0 + TT)))
                    k.stt(hv, B_OSB.v(M1, "m1"), 0.5, hv, ALU.mult, ALU.add)
            merge_all()
            nidx = l * len(tiles) + ti + 1
            if nidx < nlayers * len(tiles):
                nti = nidx % len(tiles)
                nt0, nTT, _ = tiles[nti]
                rms_rstd(nt0, nTT, nti, B_KK, KKF[:, 0:nTT])

    mark('final')
    for ti, (t0, TT, typ) in enumerate(tiles):
        NB = TT // 128
        rms_rstd(t0, TT, ti, B_RS)
        YT = B_VS.t[:, 0:KC * TT].bitcast(F32) if False else None
        for blk in range(NB):
            par = (ti * 4 + blk) % 2
            yk, ok_ = "ysb%d" % par, "yo%d" % par
            ysb = B_VS.t[:, par * 4096:par * 4096 + 2048].bitcast(F32).rearrange("p (c t) -> p c t", c=KC)
            for c in range(KC):
                k.stt(B_VS.v(ysb[:, c, :], yk), Hb.k(ti, (slice(None), c, slice(t0 + blk * 128, t0 + (blk + 1) * 128))),
                      nfv[:, c:c + 1], B_RS.v(B_RS.t[:, blk * 128:(blk + 1) * 128]), ALU.mult, ALU.mult)
            YO = B_VS.t[:, par * 4096 + 2048:par * 4096 + 4096].bitcast(F32)
            for half in range(2):
                ps = PB[par * 2 + half]
                for j in range(4):
                    c = half * 4 + j
                    k.tr(ps.v(ps.t[:, j * 128:(j + 1) * 128]), B_VS.v(ysb[:, c, :], yk), ident_f[:])
                k.copy(ACT if half else DVE, B_VS.v(YO[:, half * 512:(half + 1) * 512], ok_), ps[:])
            dst = yp[t0 + blk * 128:t0 + (blk + 1) * 128, :] if typ == "p" else ys
            k.dma(QS, dst, B_VS.v(YO, ok_))

    assert ws.used == len(ws.sched), (ws.used, len(ws.sched))
    k.finish()
    es.close()
    return nc, k


_CACHE = {}


def _consts():
    lmn_p, u_p, mask_p = _chunk_consts(64)
    lmn_s, u_s, mask_s = _chunk_consts(8)
    bd = (np.arange(128)[:, None] // 8 == np.arange(16)[None, :]).astype(np.float32)
    sel = np.zeros((4, 4, 128), np.float32)
    for i in range(4):
        sel[i, i, :] = 1.0
    return dict(sel=sel.reshape(4, 512), ident=np.eye(128, dtype=np.float32), lmn_p=lmn_p, lmn_s=lmn_s, u_p=u_p, u_s=u_s,
                mask_p=mask_p, mask_s=mask_s, bd=bd)


def make_in_maps(inp):
    f = lambda a: np.ascontiguousarray(np.asarray(a, dtype=np.float32))
    b_in = f(inp["b_in"])
    bfm = b_in.reshape(DEPTH, 72, 128).transpose(2, 0, 1).reshape(128, DEPTH * 72)
    vecs = np.stack([f(inp[n]) for n in ("norm_mix", "gnorm_a", "conv_b", "ln_g", "ln_b", "norm_ple")], axis=1)
    pvec = vecs.reshape(DEPTH, 6, 8, 128).transpose(3, 0, 1, 2).reshape(128, DEPTH * 6 * 8)
    nf = f(inp["norm_final"]).reshape(8, 128).T
    cw = f(inp["conv_w"]).reshape(DEPTH, CWID, 8, 128).transpose(3, 0, 2, 1).reshape(128, DEPTH * 8 * CWID)
    shared = dict(w_in=f(inp["w_in"]), w_br_a=f(inp["w_br_a"]), w_br_b=f(inp["w_br_b"]), w_o=f(inp["w_o"]),
                  w_pg=f(inp["w_ple_gate"]), w_ple=f(inp["w_ple"]), b_in=b_in, lb_param=f(inp["lb_param"]),
                  bfm=np.ascontiguousarray(bfm), pvec=np.ascontiguousarray(pvec), nf=np.ascontiguousarray(nf),
                  cw=np.ascontiguousarray(cw))
    shared.update(_consts())
    x_prompt, x_sample = f(inp["x_prompt"]), f(inp["x_sample"])
    state_hgrn, state_conv = f(inp["state_hgrn"]), f(inp["state_conv"])
    p_prompt, p_sample = f(inp["p_prompt"]), f(inp["p_sample"])
    maps = []
    for c in range(NCORE):
        sl = slice(c * 16, (c + 1) * 16)
        m = dict(shared)
        m["xp"] = np.ascontiguousarray(x_prompt[c])
        m["xs"] = np.ascontiguousarray(x_sample[sl].reshape(TS, D))
        m["sh"] = np.ascontiguousarray(state_hgrn[:, sl])
        m["sc"] = np.ascontiguousarray(state_conv[:, sl])
        m["pp"] = np.ascontiguousarray(p_prompt[:, c])
        m["pps"] = np.ascontiguousarray(p_sample[:, sl].reshape(DEPTH, TS, PLE))
        maps.append(m)
    return maps


def assemble(results):
    y_prompt = np.stack([r["yp"] for r in results], axis=0)
    y_sample = np.concatenate([r["ys"].reshape(16, 8, D) for r in results], axis=0)
    hgrn_p = np.stack([r["hp"] for r in results], axis=1)
    conv_p = np.stack([r["cp"] for r in results], axis=1)
    hgrn_s = np.concatenate([r["hs"] for r in results], axis=1)
    conv_s = np.concatenate([r["cs"] for r in results], axis=1)
    return tuple(np.ascontiguousarray(a, dtype=np.float32) for a in (y_prompt, y_sample, hgrn_p, conv_p, hgrn_s, conv_s))


def kernel(**inputs):
    if "nc" not in _CACHE:
        _CACHE["nc"] = build_program()[0]
    nc = _CACHE["nc"]
    maps = make_in_maps(inputs)
    res = run_bass_kernel_spmd(nc, maps, core_ids=list(range(NCORE)))
    return assemble(res.results)
```
